# Optimizing a Trainium2 kernel written in Bass

```python
import jax, jax.numpy as jnp
from jax import lax
import numpy as np

D_MODEL = 2048
BATCH = 8
SEQ = 4096
DEPTH = 2

N_A = DEPTH // 2
N_B = DEPTH - N_A
N_HEADS = 16
HEAD_DIM = D_MODEL // N_HEADS
CONV_WIDTH = 3
D_FF = 4 * D_MODEL
BLOCK_Q = 128
NORM_EPS = 1e-6

kernel_name = "yoco_shortconv_stickbreaking_sandwich"


def rmsnorm(x, g):
    x32 = x.astype(jnp.float32)
    y = x32 * lax.rsqrt(jnp.mean(jnp.square(x32), axis=-1, keepdims=True) + NORM_EPS)
    return (y * g.astype(jnp.float32)).astype(x.dtype)


def short_conv_mixer(u, w_in, conv_w, w_out):
    proj = u @ w_in
    xin, gate_c, gate_b = jnp.split(proj, 3, axis=-1)
    v = gate_c * xin
    S = v.shape[1]
    vp = jnp.pad(v, ((0, 0), (CONV_WIDTH - 1, 0), (0, 0)))
    conv = sum(conv_w[k] * vp[:, k:k + S] for k in range(CONV_WIDTH))
    return (gate_b * conv) @ w_out


def _stick_breaking_block(q_blk, k_pre, v_pre, q_start):
    Q = q_blk.shape[1]
    L = k_pre.shape[1]
    z = jnp.einsum('bqhd,bkhd->bhqk', q_blk.astype(jnp.float32), k_pre.astype(jnp.float32)) * (HEAD_DIM ** -0.5)
    t_idx = q_start + jnp.arange(Q)[:, None]
    s_idx = jnp.arange(L)[None, :]
    causal = s_idx < t_idx
    log_beta = jax.nn.log_sigmoid(z)
    log_1m = jnp.where(causal, log_beta - z, 0.0)
    incl = lax.cumsum(log_1m, axis=3, reverse=True)
    excl = jnp.concatenate([incl[..., 1:], jnp.zeros_like(incl[..., :1])], axis=-1)
    a = jnp.where(causal, jnp.exp(log_beta + excl), 0.0)
    return jnp.einsum('bhqk,bkhd->bqhd', a.astype(v_pre.dtype), v_pre)


def stick_breaking_attention(q, k, v):
    S = q.shape[1]
    outs = []
    for i in range(S // BLOCK_Q):
        start = i * BLOCK_Q
        end = start + BLOCK_Q
        outs.append(_stick_breaking_block(q[:, start:end], k[:, :end], v[:, :end], start))
    return jnp.concatenate(outs, axis=1)


def stick_breaking_mixer(u, w_q, k_sh, v_sh, w_o):
    Bsz, S, _ = u.shape
    q = (u @ w_q).reshape(Bsz, S, N_HEADS, HEAD_DIM)
    o = stick_breaking_attention(q, k_sh, v_sh)
    return o.reshape(Bsz, S, D_MODEL) @ w_o


def sq_relu_mlp(u, w_up, w_down):
    return jnp.square(jax.nn.relu(u @ w_up)) @ w_down


def setup_inputs(seed: int = 0) -> dict:
    key = jax.random.key(seed)
    ks = jax.random.split(key, 16)
    D = D_MODEL
    f32 = jnp.float32

    def nrm(k, shape, fan_in):
        return jax.random.normal(k, shape, f32) * (fan_in ** -0.5)

    def gain(k, shape):
        return 1.0 + 0.02 * jax.random.normal(k, shape, f32)

    return {
        "x": jax.random.normal(ks[0], (BATCH, SEQ, D), f32),
        "a_w_in": nrm(ks[1], (N_A, D, 3 * D), D),
        "a_conv_w": nrm(ks[2], (N_A, CONV_WIDTH, D), CONV_WIDTH),
        "a_w_out": nrm(ks[3], (N_A, D, D), D),
        "kv_norm_g": gain(ks[4], (D,)),
        "w_kv": nrm(ks[5], (D, 2 * D), D),
        "b_w_q": nrm(ks[6], (N_B, D, D), D),
        "b_w_o": nrm(ks[7], (N_B, D, D), D),
        "mix_pre_g": gain(ks[8], (DEPTH, D)),
        "mix_post_g": gain(ks[9], (DEPTH, D)),
        "mlp_pre_g": gain(ks[10], (DEPTH, D)),
        "mlp_post_g": gain(ks[11], (DEPTH, D)),
        "mlp_w_up": nrm(ks[12], (DEPTH, D, D_FF), D),
        "mlp_w_down": nrm(ks[13], (DEPTH, D_FF, D), D_FF),
    }


def reference(x, a_w_in, a_conv_w, a_w_out, kv_norm_g, w_kv, b_w_q, b_w_o,
              mix_pre_g, mix_post_g, mlp_pre_g, mlp_post_g, mlp_w_up, mlp_w_down):
    Bsz, S, _ = x.shape
    h = x
    k_sh = None
    v_sh = None
    for i in range(DEPTH):
        u = rmsnorm(h, mix_pre_g[i])
        if i < N_A:
            m = short_conv_mixer(u, a_w_in[i], a_conv_w[i], a_w_out[i])
        else:
            j = i - N_A
            m = stick_breaking_mixer(u, b_w_q[j], k_sh, v_sh, b_w_o[j])
        h = h + rmsnorm(m, mix_post_g[i])
        u = rmsnorm(h, mlp_pre_g[i])
        h = h + rmsnorm(sq_relu_mlp(u, mlp_w_up[i], mlp_w_down[i]), mlp_post_g[i])
        if i == N_A - 1:
            kv = rmsnorm(h, kv_norm_g) @ w_kv
            k_flat, v_flat = jnp.split(kv, 2, axis=-1)
            k_sh = k_flat.reshape(Bsz, S, N_HEADS, HEAD_DIM)
            v_sh = v_flat.reshape(Bsz, S, N_HEADS, HEAD_DIM)
    return h
```

```python
import os
import numpy as np
import ml_dtypes
from contextlib import ExitStack
import concourse.bass as bass
import concourse.mybir as mybir
from concourse.bass_utils import run_bass_kernel_spmd

F32 = mybir.dt.float32
BF16 = mybir.dt.bfloat16
ALU = mybir.AluOpType
AF = mybir.ActivationFunctionType

D = 2048
DC = 16
TT = 512
DFF = 8192
FC = 64
NH = 16
EPS = 1e-6
NEG = -4096.0
SCALE = 128 ** -0.5

EPOCH = 16000
ENGS = ("pe", "act", "dve", "pool", "sp")


class Buf:
    __slots__ = ("name", "w", "r", "excl")

    def __init__(self, name, excl=False):
        self.name = name
        self.excl = excl
        self.w = None
        self.r = []


class Chan:
    __slots__ = ("sem", "count", "name")

    def __init__(self, sem, name=""):
        self.sem = sem
        self.count = 0
        self.name = name


class Ins:
    __slots__ = ("eng", "fn", "waits", "signal", "chan", "sigidx")

    def __init__(self, eng, fn, chan):
        self.eng = eng
        self.fn = fn
        self.waits = []
        self.signal = False
        self.chan = chan
        self.sigidx = 0


class Sched:
    def __init__(self, nc, stack, n_epochs=6):
        self.nc = nc
        self.stack = stack
        self.q = {e: [] for e in ENGS}
        self.seen_e = {e: {} for e in ENGS}
        self.seen_c = {e: {} for e in ENGS}
        self.sems = {e: [stack.enter_context(nc.semaphore(f"s_{e}{k}")) for k in range(n_epochs)]
                     for e in ("pe", "act", "dve")}
        self.n_epochs = n_epochs
        self.all_chans = []

    def chan(self, name):
        c = Chan(self.stack.enter_context(self.nc.semaphore("c_" + name)), name)
        self.all_chans.append(c)
        return c

    def op(self, eng, fn, reads=(), writes=(), chan=None):
        q = self.q[eng]
        idx = len(q)
        ins = Ins(eng, fn, chan)
        is_dma = chan is not None
        need_e = {}
        need_c = {}
        if is_dma:
            if chan.count > 0:
                need_c[chan] = chan.count
            chan.count += 16
            ev = ("c", chan, chan.count)
        else:
            ev = ("e", eng, idx)

        def add(dep, raw):
            if dep is None:
                return
            if dep[0] == "e":
                _, e2, i2 = dep
                if e2 == eng and not is_dma:
                    if not raw or eng == "pe":
                        return
                if need_e.get(e2, -1) < i2:
                    need_e[e2] = i2
            else:
                _, c2, v2 = dep
                if need_c.get(c2, -1) < v2:
                    need_c[c2] = v2

        if any(b.excl for b in reads):
            writes = list(writes) + [b for b in reads if b.excl and b not in writes]
        for b in reads:
            add(b.w, True)
        for b in writes:
            add(b.w, False)
            for d in b.r:
                add(d, False)
        se = self.seen_e[eng]
        for e2, i2 in need_e.items():
            if se.get(e2, -1) >= i2:
                continue
            se[e2] = i2
            self.q[e2][i2].signal = True
            ins.waits.append(("e", e2, i2))
        sc = self.seen_c[eng]
        for c2, v2 in need_c.items():
            if sc.get(c2, -1) >= v2:
                continue
            sc[c2] = v2
            ins.waits.append(("c", c2, v2))
        for b in reads:
            if ev[0] == "e":
                b.r = [d for d in b.r if not (d[0] == "e" and d[1] == eng)]
            else:
                b.r = [d for d in b.r if not (d[0] == "c" and d[1] is chan)]
            b.r.append(ev)
        for b in writes:
            b.w = ev
            b.r = []
        q.append(ins)
        return ev

    def wait_events(self, eng, events):
        ins = Ins(eng, None, None)
        for ev in events:
            if ev[0] == "e":
                self.q[ev[1]][ev[2]].signal = True
            ins.waits.append(ev)
        self.q[eng].append(ins)

    def emit(self):
        nc = self.nc
        for e in ENGS:
            c = 0
            for ins in self.q[e]:
                if ins.signal and ins.chan is None and ins.fn is not None:
                    assert e in self.sems, e
                    c += 1
                    ins.sigidx = c
            assert c <= EPOCH * self.n_epochs, (e, c)
        sems = self.sems
        qs = self.q
        for e in sems:
            for sh in sems[e]:
                nc.gpsimd.sem_clear(sh)
        for c in self.all_chans:
            nc.gpsimd.sem_clear(c.sem)
        nc.all_engine_barrier()

        def run(e, engine):
            for ins in qs[e]:
                for w in ins.waits:
                    if w[0] == "e":
                        s = qs[w[1]][w[2]].sigidx
                        assert s > 0
                        engine.wait_ge(sems[w[1]][(s - 1) // EPOCH], (s - 1) % EPOCH + 1)
                    else:
                        engine.wait_ge(w[1].sem, w[2])
                if ins.fn is None:
                    continue
                bi = ins.fn(engine)
                if ins.chan is not None:
                    bi.then_inc(ins.chan.sem, 16)
                elif ins.signal:
                    s = ins.sigidx
                    bi.then_inc(sems[e][(s - 1) // EPOCH], 1)

        with nc.Block() as block:
            @block.sync
            def _(eng):
                run("sp", eng)

            @block.tensor
            def _(eng):
                run("pe", eng)

            @block.scalar
            def _(eng):
                run("act", eng)

            @block.vector
            def _(eng):
                run("dve", eng)

            @block.gpsimd
            def _(eng):
                run("pool", eng)


G_MIXPRE0, G_MIXPRE1, G_MIXPOST0, G_MIXPOST1, G_MLPPRE0, G_MLPPRE1, G_MLPPOST0, G_MLPPOST1, G_KV, G_CW0, G_CW1, G_CW2 = range(12)
NVEC = 12


def build_nc(NT):
    S = NT * TT
    NBLK = S // 128
    nc = bass.Bass("TRN2", target_bir_lowering=False)
    dt_in = lambda name, shape, dt=F32: nc.dram_tensor(name, shape, dt, kind="ExternalInput").ap()
    xin = dt_in("xin", [DC, 128, S])
    gvec = dt_in("gvec", [128, NVEC * DC])
    consts = dt_in("consts", [128, 384], BF16)
    w_in = dt_in("w_in", [D, 3 * D])
    w_out = dt_in("w_out", [D, D])
    w_kv = dt_in("w_kv", [D, 2 * D])
    w_q = dt_in("w_q", [D, D])
    w_o = dt_in("w_o", [D, D])
    w_up = dt_in("w_up", [2, D, DFF])
    w_dn = dt_in("w_dn", [2, DFF, D])
    outT = nc.dram_tensor("outT", [DC, 128, S], F32, kind="ExternalOutput").ap()
    WS = nc.dram_tensor("WS", [96, 128, DC * 512], BF16, kind="Internal").ap()
    _kvkind = "ExternalOutput" if os.environ.get("KDBG", "") != "" else "Internal"
    KT = nc.dram_tensor("dbg_KT" if _kvkind != "Internal" else "KT", [NH, 128, S], BF16, kind=_kvkind).ap()
    VS = nc.dram_tensor("dbg_VS" if _kvkind != "Internal" else "VS", [NH, 128, NBLK, 128], BF16, kind=_kvkind).ap()

    wsrc = []
    for k in range(12):
        wsrc.append(w_in[:, k * 512:(k + 1) * 512])
    for k in range(4):
        wsrc.append(w_out[:, k * 512:(k + 1) * 512])
    for k in range(16):
        wsrc.append(w_up[0, :, k * 512:(k + 1) * 512])
    for nr in range(4):
        for q in range(4):
            wsrc.append(w_dn[0, q * 2048:(q + 1) * 2048, nr * 512:(nr + 1) * 512])
    for k in range(8):
        wsrc.append(w_kv[:, k * 512:(k + 1) * 512])
    for k in range(4):
        wsrc.append(w_q[:, k * 512:(k + 1) * 512])
    for k in range(4):
        wsrc.append(w_o[:, k * 512:(k + 1) * 512])
    for k in range(16):
        wsrc.append(w_up[1, :, k * 512:(k + 1) * 512])
    for nr in range(4):
        for q in range(4):
            wsrc.append(w_dn[1, q * 2048:(q + 1) * 2048, nr * 512:(nr + 1) * 512])
    assert len(wsrc) == 96

    with ExitStack() as st:
        S_ = Sched(nc, st)
        op = S_.op
        sb = lambda name, shape, dt: st.enter_context(nc.sbuf_tensor(name, shape, dt))
        h_t = sb("h_t", [128, DC, TT], F32)
        u_t = sb("u_t", [128, DC, TT], BF16)
        m_t = sb("m_t", [128, DC, TT], F32)
        big_t = sb("big_t", [128, FC, TT], BF16)
        NWB = 3
        wt_t = [sb(f"wt{i}", [128, DC, 512], BF16) for i in range(NWB)]
        gv_t = sb("gv_t", [128, NVEC * DC], F32)
        cst_t = sb("cst_t", [128, 384], BF16)
        sq_t = [sb(f"sq{i}", [128, TT], BF16) for i in range(2)]
        rt_t = [sb(f"rt{i}", [128, TT], F32) for i in range(2)]
        rs_t = sb("rs_t", [128, TT], F32)
        rstd_t = sb("rstd_t", [128, TT], F32)
        cb_t = sb("cb_t", [128, TT], F32)
        vc_t = sb("vc_t", [128, DC, 2], F32)
        one_t = sb("one_t", [128, 1], F32)
        ps_all = st.enter_context(nc.psum_tensor("ps_all", [128, 8, 512], F32))
        ps = [ps_all[:, i, :] for i in range(8)]
        pzflat = ps_all[:, 0:4, :].rearrange("p a b -> p (a b)")
        ident = cst_t[:, 0:128]
        ones = cst_t[:, 128:256]
        maskm = cst_t[:, 256:384]

        Bh = [Buf(f"h{c}") for c in range(DC)]
        Bu = [Buf(f"u{c}") for c in range(DC)]
        Bm = [Buf(f"m{c}") for c in range(DC)]
        Bbig = [Buf(f"big{c}") for c in range(FC)]
        Bwt = [Buf(f"wt{i}") for i in range(NWB)]
        Bps = [Buf(f"ps{i}", excl=True) for i in range(8)]
        Bgv, Bcst = Buf("gv"), Buf("cst")
        Bsq = [Buf("sq0"), Buf("sq1")]
        Brt = [Buf("rt0"), Buf("rt1")]
        Brs, Brstd, Bcb, Bvc, Bone = Buf("rs"), Buf("rstd"), Buf("cb"), Buf("vc"), Buf("one")
        Bws = [Buf(f"ws{k}") for k in range(96)]
        Bkd = [Buf(f"kd{i}") for i in range(NT)]
        Bvd = [Buf(f"vd{i}") for i in range(NT)]
        Bxin = Buf("xin_dram")
        Bout = [Buf(f"out{i}") for i in range(NT)]

        ch_w = [S_.chan(f"w{i}") for i in range(NWB)]
        ch_cast = [S_.chan(f"cast{i}") for i in range(6)]
        ch_x, ch_o, ch_g, ch_c = S_.chan("x"), S_.chan("o"), S_.chan("g"), S_.chan("c")
        ch_ks, ch_vs = S_.chan("ks"), S_.chan("vs")
        ch_kl = [S_.chan("kl0"), S_.chan("kl1")]
        ch_vl = [S_.chan("vl0"), S_.chan("vl1")]

        DBG = os.environ.get("KDBG", "") != ""
        dbg_n = [0]

        def dbg(name, ap, bufs, shape, dt):
            if not DBG:
                return
            t = nc.dram_tensor("dbg_" + name, list(shape), dt, kind="ExternalOutput").ap()
            ch = S_.chan(f"dbg{dbg_n[0]}")
            dbg_n[0] += 1
            op("pool", lambda e: e.dma_start(out=t, in_=ap), reads=bufs, writes=[Buf("dbgd")], chan=ch)
            dbg_chans.append(ch)
        dbg_chans = []

        mflat = m_t[:].rearrange("p c n -> p (c n)")
        uflat32 = u_t[:].rearrange("p c n -> p (c n)").bitcast(F32)
        P_v = mflat[:, 0:4097]
        a_v = mflat[:, 4608:6656].bitcast(BF16)
        aT_v = [mflat[:, 6656 + 512 * i: 6656 + 512 * (i + 1)].bitcast(BF16).rearrange("p (a b) -> p a b", b=128)
                for i in range(2)]
        bigflat = big_t[:].rearrange("p c n -> p (c n)")
        y_v = big_t[:, 0:16, :]
        QT_v = big_t[:, 0:16, :]
        OT_v = big_t[:, 16:32, :]
        Kst_v = big_t[:, 0:16, :]
        Vst_v = bigflat[:, 16 * 512:32 * 512].rearrange("p (t n) -> p t n", n=2048)
        KTh_v = [bigflat[:, (32 + 8 * i) * 512:(40 + 8 * i) * 512] for i in range(2)]
        Vh_v = [bigflat[:, (48 + 8 * i) * 512:(56 + 8 * i) * 512].rearrange("p (b d) -> p b d", d=128) for i in range(2)]

        def bw(a, b):
            return Bu[(4 * a) // 1024:(4 * b - 1) // 1024 + 1]

        def bP(a, b):
            return Bm[(4 * a) // 2048:(4 * b - 1) // 2048 + 1]

        def ba(a, b):
            return Bm[9 + (2 * a) // 2048: 9 + (2 * b - 1) // 2048 + 1]

        BaT = [[Bm[13]], [Bm[14]]]
        BQT = Bbig[0:16]
        BOT = Bbig[16:32]
        BKTh = [Bbig[32:40], Bbig[40:48]]
        BVh = [Bbig[48:56], Bbig[56:64]]

        def gcol(vid, c):
            return gv_t[:, vid * DC + c: vid * DC + c + 1]

        op("sp", lambda e: e.dma_start(out=gv_t[:], in_=gvec), writes=[Bgv], chan=ch_g)
        op("sp", lambda e: e.dma_start(out=cst_t[:], in_=consts), writes=[Bcst], chan=ch_c)
        op("dve", lambda e: e.memset(one_t[:], 1.0), writes=[Bone])
        op("dve", lambda e: e.memset(vc_t[:].rearrange("p c n -> p (c n)"), 0.0), writes=[Bvc])

        wstate = {"cast": 0, "load": 0, "use": 0}
        total_w = 96 * NT

        def emit_cast_upto(k_lim):
            while wstate["cast"] < min(k_lim, 96):
                k = wstate["cast"]
                src = wsrc[k].rearrange("(c p) n -> p c n", p=128)
                dst = WS[k].rearrange("p (c n) -> p c n", n=512)
                op("pool", (lambda src, dst: lambda e: e.dma_start(out=dst, in_=src))(src, dst),
                   writes=[Bws[k]], chan=ch_cast[k % len(ch_cast)])
                wstate["cast"] += 1

        def emit_loads_upto(g_lim):
            while wstate["load"] < min(g_lim, total_w):
                g = wstate["load"]
                k = g % 96
                if g < 96:
                    emit_cast_upto(k + 5)
                bi = g % NWB
                op("sp", (lambda bi, k: lambda e: e.dma_start(out=wt_t[bi][:].rearrange("p c n -> p (c n)"), in_=WS[k]))(bi, k),
                   reads=[Bws[k]], writes=[Bwt[bi]], chan=ch_w[bi])
                wstate["load"] += 1

        def next_w():
            g = wstate["use"]
            emit_loads_upto(g + NWB)
            wstate["use"] += 1
            return wt_t[g % NWB], Bwt[g % NWB]

        bank_rr = {"i": 0}

        def next_bank():
            b = bank_rr["i"] % 4
            bank_rr["i"] += 1
            return b

        pending_stats = []

        def flush_stats():
            while pending_stats:
                pending_stats.pop(0)()

        def stats_from(src_ap_fn, src_bufs_fn, c, first, last, defer=False):
            sq = sq_t[c % 2]
            op("act", lambda e: e.activation(out=sq[:], in_=src_ap_fn(c), func=AF.Square),
               reads=src_bufs_fn(c), writes=[Bsq[c % 2]])

            def pe_part():
                op("pe", lambda e: e.matmul(ps[7], lhsT=ones, rhs=sq[:], start=first, stop=last),
                   reads=[Bsq[c % 2], Bcst], writes=[Bps[7]])
            if defer:
                pending_stats.append(pe_part)
            else:
                pe_part()

        def finish_rstd():
            flush_stats()
            op("act", lambda e: e.activation(out=rs_t[:], in_=ps[7], func=AF.Sqrt, scale=1.0 / D, bias=EPS),
               reads=[Bps[7]], writes=[Brs])
            op("dve", lambda e: e.reciprocal(rstd_t[:], rs_t[:]), reads=[Brs], writes=[Brstd])

        def prenorm(vid):
            for c in range(DC):
                stats_from(lambda c: h_t[:, c, :], lambda c: [Bh[c]], c, c == 0, c == DC - 1)
            finish_rstd()
            for c in range(DC):
                op("dve", (lambda c: lambda e: e.scalar_tensor_tensor(out=u_t[:, c, :], in0=h_t[:, c, :], scalar=gcol(vid, c),
                                                                       in1=rstd_t[:], op0=ALU.mult, op1=ALU.mult))(c),
                   reads=[Bh[c], Bgv, Brstd], writes=[Bu[c]])

        def linear_fm(n_wtiles, rhs_fn, rhs_bufs_fn, evac):
            for wtile in range(n_wtiles):
                wt, bwt = next_w()
                for j in range(4):
                    b = next_bank()
                    for kc in range(DC):
                        op("pe", (lambda wt, j, kc, b: lambda e: e.matmul(ps[b], lhsT=wt[:, kc, j * 128:(j + 1) * 128], rhs=rhs_fn(kc),
                                                                          start=(kc == 0), stop=(kc == DC - 1)))(wt, j, kc, b),
                           reads=[bwt] + rhs_bufs_fn(kc), writes=[Bps[b]])
                    flush_stats()
                    evac(wtile * 4 + j, b)

        def evac_postnorm(vid):
            def f(c, b):
                op("act", lambda e: e.activation(out=m_t[:, c, :], in_=ps[b], func=AF.Identity, scale=gcol(vid, c)),
                   reads=[Bps[b], Bgv], writes=[Bm[c]])
                stats_from(lambda c_: ps[b], lambda c_: [Bps[b]], c, c == 0, c == DC - 1, defer=True)
            return f

        def residual_update():
            finish_rstd()
            for c in range(DC):
                tmp = rt_t[c % 2]
                op("dve", (lambda c, tmp: lambda e: e.tensor_tensor(out=tmp[:], in0=m_t[:, c, :], in1=rstd_t[:], op=ALU.mult))(c, tmp),
                   reads=[Bm[c], Brstd], writes=[Brt[c % 2]])
                op("dve", (lambda c, tmp: lambda e: e.tensor_tensor(out=h_t[:, c, :], in0=h_t[:, c, :], in1=tmp[:], op=ALU.add))(c, tmp),
                   reads=[Bh[c], Brt[c % 2]], writes=[Bh[c]])

        def mlp(layer):
            prenorm(G_MLPPRE0 + layer)

            def evac_up(f, b):
                tmp = rt_t[f % 2]
                op("act", lambda e: e.activation(out=tmp[:], in_=ps[b], func=AF.Relu), reads=[Bps[b]], writes=[Brt[f % 2]])
                op("dve", lambda e: e.tensor_tensor(out=big_t[:, f, :], in0=tmp[:], in1=tmp[:], op=ALU.mult),
                   reads=[Brt[f % 2]], writes=[Bbig[f]])
            linear_fm(16, lambda kc: u_t[:, kc, :], lambda kc: [Bu[kc]], evac_up)
            ev = evac_postnorm(G_MLPPOST0 + layer)
            for nr in range(4):
                for q in range(4):
                    wt, bwt = next_w()
                    for j in range(4):
                        for kc in range(DC):
                            f = q * 16 + kc
                            op("pe", (lambda wt, j, kc, f, q: lambda e: e.matmul(ps[j], lhsT=wt[:, kc, j * 128:(j + 1) * 128], rhs=big_t[:, f, :],
                                                                                  start=(q == 0 and kc == 0), stop=(q == 3 and kc == DC - 1)))(wt, j, kc, f, q),
                               reads=[bwt, Bbig[f]], writes=[Bps[j]])
                        if q == 3:
                            flush_stats()
                            ev(nr * 4 + j, j)
            bank_rr["i"] = 0
            residual_update()

        for ti in range(NT):
            Rt = S - (ti + 1) * TT
            Lt = (ti + 1) * TT
            blk_lo = Rt // 128
            op("sp", (lambda Rt: lambda e: e.dma_start(out=h_t[:], in_=xin[:, :, Rt:Rt + TT].rearrange("c p r -> p c r")))(Rt),
               reads=[Bxin], writes=Bh, chan=ch_x)

            prenorm(G_MIXPRE0)

            def evac_in(cidx, b, ti=ti):
                grp, j = divmod(cidx, DC)
                if grp == 0:
                    op("act", lambda e: e.activation(out=m_t[:, j, :], in_=ps[b], func=AF.Identity),
                       reads=[Bps[b]], writes=[Bm[j]])
                elif grp == 1:
                    op("dve", lambda e: e.tensor_tensor(out=m_t[:, j, :], in0=ps[b], in1=m_t[:, j, :], op=ALU.mult),
                       reads=[Bps[b], Bm[j]], writes=[Bm[j]])
                else:
                    v = m_t[:, j, :]
                    op("dve", lambda e: e.tensor_scalar(out=cb_t[:], in0=v, scalar1=gcol(G_CW2, j), scalar2=None, op0=ALU.mult),
                       reads=[Bm[j], Bgv], writes=[Bcb])
                    op("dve", lambda e: e.scalar_tensor_tensor(out=cb_t[:, 0:TT - 1], in0=m_t[:, j, 1:TT], scalar=gcol(G_CW1, j),
                                                               in1=cb_t[:, 0:TT - 1], op0=ALU.mult, op1=ALU.add),
                       reads=[Bm[j], Bgv, Bcb], writes=[Bcb])
                    op("dve", lambda e: e.scalar_tensor_tensor(out=cb_t[:, 0:TT - 2], in0=m_t[:, j, 2:TT], scalar=gcol(G_CW0, j),
                                                               in1=cb_t[:, 0:TT - 2], op0=ALU.mult, op1=ALU.add),
                       reads=[Bm[j], Bgv, Bcb], writes=[Bcb])
                    if ti > 0:
                        op("dve", lambda e: e.scalar_tensor_tensor(out=cb_t[:, TT - 1:TT], in0=vc_t[:, j, 0:1], scalar=gcol(G_CW1, j),
                                                                   in1=cb_t[:, TT - 1:TT], op0=ALU.mult, op1=ALU.add),
                           reads=[Bvc, Bgv, Bcb], writes=[Bcb])
                        op("dve", lambda e: e.scalar_tensor_tensor(out=cb_t[:, TT - 2:TT], in0=vc_t[:, j, 0:2], scalar=gcol(G_CW0, j),
                                                                   in1=cb_t[:, TT - 2:TT], op0=ALU.mult, op1=ALU.add),
                           reads=[Bvc, Bgv, Bcb], writes=[Bcb])
                    op("dve", lambda e: e.tensor_copy(vc_t[:, j, :], m_t[:, j, 0:2]), reads=[Bm[j], Bcb], writes=[Bvc])
                    op("dve", lambda e: e.tensor_tensor(out=y_v[:, j, :], in0=ps[b], in1=cb_t[:], op=ALU.mult),
                       reads=[Bps[b], Bcb], writes=[Bbig[j]])
            linear_fm(12, lambda kc: u_t[:, kc, :], lambda kc: [Bu[kc]], evac_in)
            linear_fm(4, lambda kc: y_v[:, kc, :], lambda kc: [Bbig[kc]], evac_postnorm(G_MIXPOST0))
            residual_update()
            mlp(0)

            prenorm(G_KV)

            def evac_k(c, b):
                op("act", lambda e: e.activation(out=Kst_v[:, c, :], in_=ps[b], func=AF.Identity),
                   reads=[Bps[b]], writes=[Bbig[c]])
            linear_fm(4, lambda kc: u_t[:, kc, :], lambda kc: [Bu[kc]], evac_k)
            op("pool", (lambda Rt: lambda e: e.dma_start(out=KT[:, :, Rt:Rt + TT].rearrange("h p r -> p h r"), in_=Kst_v))(Rt),
               reads=Bbig[0:16], writes=[Bkd[ti]], chan=ch_ks)
            if ti == 0:
                dbg("Kst", Kst_v, Bbig[0:16], [128, 16, 512], BF16)
            for g4 in range(4):
                wt, bwt = next_w()
                for tb in range(4):
                    b = next_bank()
                    for kc in range(DC):
                        op("pe", (lambda wt, kc, tb, b: lambda e: e.matmul(ps[b], lhsT=u_t[:, kc, tb * 128:(tb + 1) * 128], rhs=wt[:, kc, :],
                                                                           start=(kc == 0), stop=(kc == DC - 1)))(wt, kc, tb, b),
                           reads=[bwt, Bu[kc]], writes=[Bps[b]])
                    if tb % 2 == 0:
                        op("act", (lambda tb, g4, b: lambda e: e.activation(out=Vst_v[:, tb, g4 * 512:(g4 + 1) * 512], in_=ps[b], func=AF.Identity))(tb, g4, b),
                           reads=[Bps[b]], writes=Bbig[16 + 4 * tb + g4:16 + 4 * tb + g4 + 1])
                    else:
                        op("dve", (lambda tb, g4, b: lambda e: e.tensor_copy(Vst_v[:, tb, g4 * 512:(g4 + 1) * 512], ps[b]))(tb, g4, b),
                           reads=[Bps[b]], writes=Bbig[16 + 4 * tb + g4:16 + 4 * tb + g4 + 1])
            for tb in range(4):
                blk = blk_lo + tb
                op("pool", (lambda tb, blk: lambda e: e.dma_start(out=VS[:, :, blk, :].rearrange("h p d -> p h d"),
                                                                  in_=Vst_v[:, tb, :].rearrange("p (h d) -> p h d", d=128)))(tb, blk),
                   reads=Bbig[16 + 4 * tb:16 + 4 * tb + 4], writes=[Bvd[ti]], chan=ch_vs)

            prenorm(G_MIXPRE1)

            def evac_q(c, b):
                op("act", lambda e: e.activation(out=QT_v[:, c, :], in_=ps[b], func=AF.Identity),
                   reads=[Bps[b]], writes=[Bbig[c]])
            linear_fm(4, lambda kc: u_t[:, kc, :], lambda kc: [Bu[kc]], evac_q)

            nvb = Lt // 128
            if ti == 0:
                dbg("QT", QT_v, Bbig[0:16], [128, 16, 512], BF16)
                dbg("h1", h_t[:], Bh, [128, 16, 512], F32)

            def load_head(hh, slot, Rt=Rt, Lt=Lt, nvb=nvb, blk_lo=blk_lo, ti=ti):
                op("sp", lambda e: e.dma_start(out=KTh_v[slot][:, 0:Lt], in_=KT[hh, :, Rt:S]),
                   reads=Bkd[0:ti + 1], writes=BKTh[slot], chan=ch_kl[slot])
                op("sp", lambda e: e.dma_start(out=Vh_v[slot][:, 0:nvb, :], in_=VS[hh, :, blk_lo:NBLK, :]),
                   reads=Bvd[0:ti + 1], writes=BVh[slot], chan=ch_vl[slot])

            units = []
            for hh in range(NH):
                for qb in range(4):
                    k0 = 128 * qb
                    L = Lt - k0
                    npieces = (L + 511) // 512
                    for g0p in range(0, npieces, 4):
                        gp = min(4, npieces - g0p)
                        g0 = g0p * 512
                        gn = min(L - g0, 2048)
                        units.append(dict(hh=hh, slot=hh % 2, qb=qb, k0=k0, L=L, g0=g0, gp=gp, gn=gn,
                                          first=(g0p == 0), last=(g0 + gn >= L), par=len(units) % 2))
            tslot = [0]

            def stageA(u):
                hh, slot, k0, L, g0, gp, gn = u["hh"], u["slot"], u["k0"], u["L"], u["g0"], u["gp"], u["gn"]
                qT = QT_v[:, hh, k0:k0 + 128]
                kth = KTh_v[slot]
                for pi in range(gp):
                    c0 = g0 + pi * 512
                    n = min(512, L - c0)
                    diag = (c0 == 0)
                    op("pe", (lambda pi, c0, n, diag: lambda e: e.matmul(ps[pi][:, 0:n], lhsT=qT, rhs=kth[:, k0 + c0:k0 + c0 + n],
                                                                         start=True, stop=not diag))(pi, c0, n, diag),
                       reads=[Bbig[hh]] + BKTh[slot], writes=[Bps[pi]])
                    if diag:
                        op("pe", lambda e: e.matmul(ps[0][:, 0:128], lhsT=ident, rhs=maskm, start=False, stop=True),
                           reads=[Bcst], writes=[Bps[0]])
                wo = 2048 * u["par"]
                op("act", lambda e: e.activation(out=uflat32[:, wo:wo + gn], in_=pzflat[:, 0:gn], func=AF.Sigmoid, scale=-SCALE),
                   reads=Bps[0:gp], writes=bw(wo, wo + gn))

            def stageB(u):
                g0, gn = u["g0"], u["gn"]
                if u["first"]:
                    op("dve", lambda e: e.tensor_copy(P_v[:, 0:1], one_t[:]), reads=[Bone], writes=bP(0, 1))
                wo = 2048 * u["par"]
                op("dve", lambda e: e.tensor_tensor_scan(out=P_v[:, 1 + g0:1 + g0 + gn], data0=uflat32[:, wo:wo + gn],
                                                         data1=uflat32[:, wo:wo + gn], initial=P_v[:, g0:g0 + 1],
                                                         op0=ALU.mult, op1=ALU.min),
                   reads=bw(wo, wo + gn) + bP(g0, g0 + 1), writes=bP(1 + g0, 1 + g0 + gn))
                op("dve", lambda e: e.tensor_tensor(out=a_v[:, g0:g0 + gn], in0=P_v[:, g0:g0 + gn], in1=P_v[:, g0 + 1:g0 + gn + 1],
                                                    op=ALU.subtract),
                   reads=bP(g0, g0 + gn + 1), writes=ba(g0, g0 + gn))

            def stageC(u):
                hh, slot, qb, k0, L, g0, gn = u["hh"], u["slot"], u["qb"], u["k0"], u["L"], u["g0"], u["gn"]
                vh = Vh_v[slot]
                nkb = gn // 128
                for t0 in range(0, nkb, 8):
                    tn = min(8, nkb - t0)
                    tb_ = 4 + (tslot[0] % 2)
                    at = aT_v[tslot[0] % 2]
                    bat = BaT[tslot[0] % 2]
                    tslot[0] += 1
                    ptv = ps[tb_].bitcast(BF16).rearrange("p (a b) -> p a b", b=128)
                    for i in range(tn):
                        cc = g0 + (t0 + i) * 128
                        op("pe", (lambda i, cc, ptv: lambda e: e.transpose(ptv[:, i, :], a_v[:, cc:cc + 128], ident))(i, cc, ptv),
                           reads=ba(cc, cc + 128) + [Bcst], writes=[Bps[tb_]])
                    op("act", (lambda at, ptv, tn: lambda e: e.activation(out=at[:, 0:tn, :], in_=ptv[:, 0:tn, :], func=AF.Identity))(at, ptv, tn),
                       reads=[Bps[tb_]], writes=bat)
                    for i in range(tn):
                        kb = t0 + i
                        vb = (k0 + g0) // 128 + kb
                        fa = (u["first"] and kb == 0)
                        la = (g0 + (kb + 1) * 128 >= L)
                        op("pe", (lambda i, vb, fa, la, at: lambda e: e.matmul(ps[6][:, k0:k0 + 128], lhsT=vh[:, vb, :], rhs=at[:, i, :],
                                                                               start=fa, stop=la))(i, vb, fa, la, at),
                           reads=BVh[slot] + bat, writes=[Bps[6]])
                if u["last"] and qb == 3:
                    op("act", lambda e: e.activation(out=OT_v[:, hh, :], in_=ps[6], func=AF.Identity),
                       reads=[Bps[6]], writes=[Bbig[16 + hh]])

            load_head(0, 0)
            stageA(units[0])
            for i, u in enumerate(units):
                if u["qb"] == 0 and u["first"] and u["hh"] + 1 < NH:
                    load_head(u["hh"] + 1, (u["hh"] + 1) % 2)
                if i + 1 < len(units):
                    stageA(units[i + 1])
                stageB(u)
                if ti == 0 and i == 0:
                    dbg("w0", uflat32[:, 0:512], bw(0, 512), [128, 512], F32)
                    dbg("P0", P_v[:, 0:513], bP(0, 513), [128, 513], F32)
                    dbg("a0", a_v[:, 0:512], ba(0, 512), [128, 512], BF16)
                    dbg("K0", KTh_v[0][:, 0:512], BKTh[0], [128, 512], BF16)
                    dbg("V0", Vh_v[0][:, 0:4, :], BVh[0], [128, 4, 128], BF16)
                stageC(u)
                if ti == 0 and i == 0:
                    dbg("aT0", aT_v[0], BaT[0], [128, 8, 128], BF16)
            if ti == 0:
                dbg("OT", OT_v, BOT, [128, 16, 512], BF16)
            bank_rr["i"] = 0
            linear_fm(4, lambda kc: OT_v[:, kc, :], lambda kc: [Bbig[16 + kc]], evac_postnorm(G_MIXPOST1))
            residual_update()
            mlp(1)
            op("pool", (lambda Rt: lambda e: e.dma_start(out=outT[:, :, Rt:Rt + TT].rearrange("c p r -> p c r"), in_=h_t[:]))(Rt),
               reads=Bh, writes=[Bout[ti]], chan=ch_o)
        S_.wait_events("pool", [("c", ch_o, ch_o.count)] + [("c", c, c.count) for c in dbg_chans])
        S_.emit()
    return nc


def _host_consts():
    c = np.zeros((128, 384), np.float32)
    c[:, 0:128] = np.eye(128)
    c[:, 128:256] = 1.0
    rt = np.arange(128)[:, None]
    rs = np.arange(128)[None, :]
    c[:, 256:384] = np.where(rs <= rt, NEG, 0.0)
    return c.astype(ml_dtypes.bfloat16)


def _gvec(mix_pre_g, mix_post_g, mlp_pre_g, mlp_post_g, kv_norm_g, a_conv_w):
    vecs = [mix_pre_g[0], mix_pre_g[1], mix_post_g[0], mix_post_g[1], mlp_pre_g[0], mlp_pre_g[1],
            mlp_post_g[0], mlp_post_g[1], kv_norm_g, a_conv_w[0, 0], a_conv_w[0, 1], a_conv_w[0, 2]]
    g = np.stack([np.asarray(v, np.float32).reshape(DC, 128).T for v in vecs], axis=1)
    return np.ascontiguousarray(g.reshape(128, NVEC * DC))


_NC_CACHE = {}


def run_cores(x, a_w_in, a_conv_w, a_w_out, kv_norm_g, w_kv, b_w_q, b_w_o,
              mix_pre_g, mix_post_g, mlp_pre_g, mlp_post_g, mlp_w_up, mlp_w_down):
    x = np.asarray(x, np.float32)
    B, S, _ = x.shape
    NT = S // TT
    if NT not in _NC_CACHE:
        _NC_CACHE[NT] = build_nc(NT)
    nc = _NC_CACHE[NT]
    shared = {
        "gvec": _gvec(*(np.asarray(v, np.float32) for v in (mix_pre_g, mix_post_g, mlp_pre_g, mlp_post_g, kv_norm_g, a_conv_w))),
        "consts": _host_consts(),
        "w_in": np.ascontiguousarray(np.asarray(a_w_in, np.float32)[0]),
        "w_out": np.ascontiguousarray(np.asarray(a_w_out, np.float32)[0]),
        "w_kv": np.ascontiguousarray(np.asarray(w_kv, np.float32)),
        "w_q": np.ascontiguousarray(np.asarray(b_w_q, np.float32)[0]),
        "w_o": np.ascontiguousarray(np.asarray(b_w_o, np.float32)[0]),
        "w_up": np.ascontiguousarray(np.asarray(mlp_w_up, np.float32)),
        "w_dn": np.ascontiguousarray(np.asarray(mlp_w_down, np.float32)),
    }
    in_maps = []
    for b in range(B):
        xr = np.ascontiguousarray(x[b, ::-1, :].T).reshape(DC, 128, S)
        m = dict(shared)
        m["xin"] = xr
        in_maps.append(m)
    res = run_bass_kernel_spmd(nc, in_maps, core_ids=list(range(B)))
    global LAST_RES
    LAST_RES = res.results
    out = np.empty((B, S, D), np.float32)
    for b in range(B):
        o = res.results[b]["outT"].reshape(D, S)
        out[b] = o.T[::-1, :]
    return out


def kernel(**inputs):
    return run_cores(**inputs)
```

```python
import os
import numpy as np
import ml_dtypes
from contextlib import ExitStack
import concourse.bass as bass
import concourse.mybir as mybir
from concourse.bass_utils import run_bass_kernel_spmd

F32 = mybir.dt.float32
BF16 = mybir.dt.bfloat16
ALU = mybir.AluOpType
AF = mybir.ActivationFunctionType

D = 2048
DC = 16
TT = 512
DFF = 8192
FC = 64
NH = 16
EPS = 1e-6
NEG = -4096.0
SCALE = 128 ** -0.5

EPOCH = 16000
ENGS = ("pe", "act", "dve", "pool", "sp")


class Buf:
    __slots__ = ("name", "w", "r", "excl")

    def __init__(self, name, excl=False):
        self.name = name
        self.excl = excl
        self.w = None
        self.r = []


class Chan:
    __slots__ = ("sem", "count", "name")

    def __init__(self, sem, name=""):
        self.sem = sem
        self.count = 0
        self.name = name


class Ins:
    __slots__ = ("eng", "fn", "waits", "signal", "chan", "sigidx")

    def __init__(self, eng, fn, chan):
        self.eng = eng
        self.fn = fn
        self.waits = []
        self.signal = False
        self.chan = chan
        self.sigidx = 0


class Sched:
    def __init__(self, nc, stack, n_epochs=6):
        self.nc = nc
        self.stack = stack
        self.q = {e: [] for e in ENGS}
        self.seen_e = {e: {} for e in ENGS}
        self.seen_c = {e: {} for e in ENGS}
        self.sems = {e: [stack.enter_context(nc.semaphore(f"s_{e}{k}")) for k in range(n_epochs)]
                     for e in ("pe", "act", "dve")}
        self.n_epochs = n_epochs
        self.all_chans = []

    def chan(self, name):
        c = Chan(self.stack.enter_context(self.nc.semaphore("c_" + name)), name)
        self.all_chans.append(c)
        return c

    def op(self, eng, fn, reads=(), writes=(), chan=None):
        q = self.q[eng]
        idx = len(q)
        ins = Ins(eng, fn, chan)
        is_dma = chan is not None
        need_e = {}
        need_c = {}
        if is_dma:
            if chan.count > 0:
                need_c[chan] = chan.count
            chan.count += 16
            ev = ("c", chan, chan.count)
        else:
            ev = ("e", eng, idx)

        def add(dep, raw):
            if dep is None:
                return
            if dep[0] == "e":
                _, e2, i2 = dep
                if e2 == eng and not is_dma:
                    if not raw or eng == "pe":
                        return
                if need_e.get(e2, -1) < i2:
                    need_e[e2] = i2
            else:
                _, c2, v2 = dep
                if need_c.get(c2, -1) < v2:
                    need_c[c2] = v2

        if any(b.excl for b in reads):
            writes = list(writes) + [b for b in reads if b.excl and b not in writes]
        for b in reads:
            add(b.w, True)
        for b in writes:
            add(b.w, False)
            for d in b.r:
                add(d, False)
        se = self.seen_e[eng]
        for e2, i2 in need_e.items():
            if se.get(e2, -1) >= i2:
                continue
            se[e2] = i2
            self.q[e2][i2].signal = True
            ins.waits.append(("e", e2, i2))
        sc = self.seen_c[eng]
        for c2, v2 in need_c.items():
            if sc.get(c2, -1) >= v2:
                continue
            sc[c2] = v2
            ins.waits.append(("c", c2, v2))
        for b in reads:
            if ev[0] == "e":
                b.r = [d for d in b.r if not (d[0] == "e" and d[1] == eng)]
            else:
                b.r = [d for d in b.r if not (d[0] == "c" and d[1] is chan)]
            b.r.append(ev)
        for b in writes:
            b.w = ev
            b.r = []
        q.append(ins)
        return ev

    def wait_events(self, eng, events):
        ins = Ins(eng, None, None)
        for ev in events:
            if ev[0] == "e":
                self.q[ev[1]][ev[2]].signal = True
            ins.waits.append(ev)
        self.q[eng].append(ins)

    def emit(self):
        nc = self.nc
        for e in ENGS:
            c = 0
            for ins in self.q[e]:
                if ins.signal and ins.chan is None and ins.fn is not None:
                    assert e in self.sems, e
                    c += 1
                    ins.sigidx = c
            assert c <= EPOCH * self.n_epochs, (e, c)
        sems = self.sems
        qs = self.q
        for e in sems:
            for sh in sems[e]:
                nc.gpsimd.sem_clear(sh)
        for c in self.all_chans:
            nc.gpsimd.sem_clear(c.sem)
        nc.all_engine_barrier()

        def run(e, engine):
            for ins in qs[e]:
                for w in ins.waits:
                    if w[0] == "e":
                        s = qs[w[1]][w[2]].sigidx
                        assert s > 0
                        engine.wait_ge(sems[w[1]][(s - 1) // EPOCH], (s - 1) % EPOCH + 1)
                    else:
                        engine.wait_ge(w[1].sem, w[2])
                if ins.fn is None:
                    continue
                bi = ins.fn(engine)
                if ins.chan is not None:
                    bi.then_inc(ins.chan.sem, 16)
                elif ins.signal:
                    s = ins.sigidx
                    bi.then_inc(sems[e][(s - 1) // EPOCH], 1)

        with nc.Block() as block:
            @block.sync
            def _(eng):
                run("sp", eng)

            @block.tensor
            def _(eng):
                run("pe", eng)

            @block.scalar
            def _(eng):
                run("act", eng)

            @block.vector
            def _(eng):
                run("dve", eng)

            @block.gpsimd
            def _(eng):
                run("pool", eng)


G_MIXPRE0, G_MIXPRE1, G_MIXPOST0, G_MIXPOST1, G_MLPPRE0, G_MLPPRE1, G_MLPPOST0, G_MLPPOST1, G_KV, G_CW0, G_CW1, G_CW2 = range(12)
NVEC = 12


def build_nc(NT):
    S = NT * TT
    NBLK = S // 128
    nc = bass.Bass("TRN2", target_bir_lowering=False)
    dt_in = lambda name, shape, dt=F32: nc.dram_tensor(name, shape, dt, kind="ExternalInput").ap()
    xin = dt_in("xin", [DC, 128, S])
    gvec = dt_in("gvec", [128, NVEC * DC])
    consts = dt_in("consts", [128, 384], BF16)
    w_in = dt_in("w_in", [D, 3 * D])
    w_out = dt_in("w_out", [D, D])
    w_kv = dt_in("w_kv", [D, 2 * D])
    w_q = dt_in("w_q", [D, D])
    w_o = dt_in("w_o", [D, D])
    w_up = dt_in("w_up", [2, D, DFF])
    w_dn = dt_in("w_dn", [2, DFF, D])
    outT = nc.dram_tensor("outT", [DC, 128, S], F32, kind="ExternalOutput").ap()
    WS = nc.dram_tensor("WS", [96, 128, DC * 512], BF16, kind="Internal").ap()
    _kvkind = "ExternalOutput" if os.environ.get("KDBG", "") != "" else "Internal"
    KT = nc.dram_tensor("dbg_KT" if _kvkind != "Internal" else "KT", [NH, 128, S], BF16, kind=_kvkind).ap()
    VS = nc.dram_tensor("dbg_VS" if _kvkind != "Internal" else "VS", [NH, 128, NBLK, 128], BF16, kind=_kvkind).ap()

    wsrc = []
    for k in range(12):
        wsrc.append(w_in[:, k * 512:(k + 1) * 512])
    for k in range(4):
        wsrc.append(w_out[:, k * 512:(k + 1) * 512])
    for k in range(16):
        wsrc.append(w_up[0, :, k * 512:(k + 1) * 512])
    for nr in range(4):
        for q in range(4):
            wsrc.append(w_dn[0, q * 2048:(q + 1) * 2048, nr * 512:(nr + 1) * 512])
    for k in range(8):
        wsrc.append(w_kv[:, k * 512:(k + 1) * 512])
    for k in range(4):
        wsrc.append(w_q[:, k * 512:(k + 1) * 512])
    for k in range(4):
        wsrc.append(w_o[:, k * 512:(k + 1) * 512])
    for k in range(16):
        wsrc.append(w_up[1, :, k * 512:(k + 1) * 512])
    for nr in range(4):
        for q in range(4):
            wsrc.append(w_dn[1, q * 2048:(q + 1) * 2048, nr * 512:(nr + 1) * 512])
    assert len(wsrc) == 96

    with ExitStack() as st:
        S_ = Sched(nc, st)
        op = S_.op
        sb = lambda name, shape, dt: st.enter_context(nc.sbuf_tensor(name, shape, dt))
        h_t = sb("h_t", [128, DC, TT], F32)
        u_t = sb("u_t", [128, DC, TT], BF16)
        m_t = sb("m_t", [128, DC, TT], F32)
        big_t = sb("big_t", [128, FC, TT], BF16)
        NWB = 3
        wt_t = [sb(f"wt{i}", [128, DC, 512], BF16) for i in range(NWB)]
        gv_t = sb("gv_t", [128, NVEC * DC], F32)
        cst_t = sb("cst_t", [128, 384], BF16)
        sq_t = [sb(f"sq{i}", [128, TT], BF16) for i in range(2)]
        rt_t = [sb(f"rt{i}", [128, TT], F32) for i in range(2)]
        rs_t = sb("rs_t", [128, TT], F32)
        rstd_t = sb("rstd_t", [128, TT], F32)
        cb_t = sb("cb_t", [128, TT], F32)
        vc_t = sb("vc_t", [128, DC, 2], F32)
        one_t = sb("one_t", [128, 1], F32)
        ps_all = st.enter_context(nc.psum_tensor("ps_all", [128, 8, 512], F32))
        ps = [ps_all[:, i, :] for i in range(8)]
        pzflat = ps_all[:, 0:4, :].rearrange("p a b -> p (a b)")
        ident = cst_t[:, 0:128]
        ones = cst_t[:, 128:256]
        maskm = cst_t[:, 256:384]

        Bh = [Buf(f"h{c}") for c in range(DC)]
        Bu = [Buf(f"u{c}") for c in range(DC)]
        Bm = [Buf(f"m{c}") for c in range(DC)]
        Bbig = [Buf(f"big{c}") for c in range(FC)]
        Bwt = [Buf(f"wt{i}") for i in range(NWB)]
        Bps = [Buf(f"ps{i}", excl=True) for i in range(8)]
        Bgv, Bcst = Buf("gv"), Buf("cst")
        Bsq = [Buf("sq0"), Buf("sq1")]
        Brt = [Buf("rt0"), Buf("rt1")]
        Brs, Brstd, Bcb, Bvc, Bone = Buf("rs"), Buf("rstd"), Buf("cb"), Buf("vc"), Buf("one")
        Bws = [Buf(f"ws{k}") for k in range(96)]
        Bkd = [Buf(f"kd{i}") for i in range(NT)]
        Bvd = [Buf(f"vd{i}") for i in range(NT)]
        Bxin = Buf("xin_dram")
        Bout = [Buf(f"out{i}") for i in range(NT)]

        ch_w = [S_.chan(f"w{i}") for i in range(NWB)]
        ch_cast = [S_.chan(f"cast{i}") for i in range(6)]
        ch_g, ch_c = S_.chan("g"), S_.chan("c")
        ch_xs = [S_.chan(f"x{i}") for i in range(4)]
        ch_os = [S_.chan(f"o{i}") for i in range(4)]
        ch_ks, ch_vs = S_.chan("ks"), S_.chan("vs")
        ch_kl = [S_.chan("kl0"), S_.chan("kl1")]
        ch_vl = [S_.chan("vl0"), S_.chan("vl1")]

        DBG = os.environ.get("KDBG", "") != ""
        dbg_n = [0]

        def dbg(name, ap, bufs, shape, dt):
            if not DBG:
                return
            t = nc.dram_tensor("dbg_" + name, list(shape), dt, kind="ExternalOutput").ap()
            ch = S_.chan(f"dbg{dbg_n[0]}")
            dbg_n[0] += 1
            op("pool", lambda e: e.dma_start(out=t, in_=ap), reads=bufs, writes=[Buf("dbgd")], chan=ch)
            dbg_chans.append(ch)
        dbg_chans = []

        mflat = m_t[:].rearrange("p c n -> p (c n)")
        uflat32 = u_t[:].rearrange("p c n -> p (c n)").bitcast(F32)
        P_v = mflat[:, 0:4097]
        a_v = mflat[:, 4608:6656].bitcast(BF16)
        aT_v = [mflat[:, 6656 + 512 * i: 6656 + 512 * (i + 1)].bitcast(BF16).rearrange("p (a b) -> p a b", b=128)
                for i in range(2)]
        bigflat = big_t[:].rearrange("p c n -> p (c n)")
        y_v = big_t[:, 0:16, :]
        QT_v = big_t[:, 0:16, :]
        OT_v = big_t[:, 16:32, :]
        Kst_v = big_t[:, 0:16, :]
        Vst_v = bigflat[:, 16 * 512:32 * 512].rearrange("p (t n) -> p t n", n=2048)
        KTh_v = [bigflat[:, (32 + 8 * i) * 512:(40 + 8 * i) * 512] for i in range(2)]
        Vh_v = [bigflat[:, (48 + 8 * i) * 512:(56 + 8 * i) * 512].rearrange("p (b d) -> p b d", d=128) for i in range(2)]

        def bw(a, b):
            return Bu[(4 * a) // 1024:(4 * b - 1) // 1024 + 1]

        def bP(a, b):
            return Bm[(4 * a) // 2048:(4 * b - 1) // 2048 + 1]

        def ba(a, b):
            return Bm[9 + (2 * a) // 2048: 9 + (2 * b - 1) // 2048 + 1]

        BaT = [[Bm[13]], [Bm[14]]]
        BQT = Bbig[0:16]
        BOT = Bbig[16:32]
        BKTh = [Bbig[32:40], Bbig[40:48]]
        BVh = [Bbig[48:56], Bbig[56:64]]

        def gcol(vid, c):
            return gv_t[:, vid * DC + c: vid * DC + c + 1]

        op("sp", lambda e: e.dma_start(out=gv_t[:], in_=gvec), writes=[Bgv], chan=ch_g)
        op("sp", lambda e: e.dma_start(out=cst_t[:], in_=consts), writes=[Bcst], chan=ch_c)
        op("dve", lambda e: e.memset(one_t[:], 1.0), writes=[Bone])
        op("dve", lambda e: e.memset(vc_t[:].rearrange("p c n -> p (c n)"), 0.0), writes=[Bvc])

        wstate = {"cast": 0, "load": 0, "use": 0}
        total_w = 96 * NT

        def emit_cast_upto(k_lim):
            while wstate["cast"] < min(k_lim, 96):
                k = wstate["cast"]
                src = wsrc[k].rearrange("(c p) n -> p c n", p=128)
                dst = WS[k].rearrange("p (c n) -> p c n", n=512)
                op("pool", (lambda src, dst: lambda e: e.dma_start(out=dst, in_=src))(src, dst),
                   writes=[Bws[k]], chan=ch_cast[k % len(ch_cast)])
                wstate["cast"] += 1

        def emit_loads_upto(g_lim):
            while wstate["load"] < min(g_lim, total_w):
                g = wstate["load"]
                k = g % 96
                if g < 96:
                    emit_cast_upto(k + 5)
                bi = g % NWB
                op("sp", (lambda bi, k: lambda e: e.dma_start(out=wt_t[bi][:].rearrange("p c n -> p (c n)"), in_=WS[k]))(bi, k),
                   reads=[Bws[k]], writes=[Bwt[bi]], chan=ch_w[bi])
                wstate["load"] += 1

        def next_w():
            g = wstate["use"]
            emit_loads_upto(g + NWB)
            wstate["use"] += 1
            return wt_t[g % NWB], Bwt[g % NWB]

        bank_rr = {"i": 0}

        def next_bank():
            b = bank_rr["i"] % 4
            bank_rr["i"] += 1
            return b

        pending_stats = []

        def flush_stats():
            while pending_stats:
                pending_stats.pop(0)()

        def stats_from(src_ap_fn, src_bufs_fn, c, first, last, defer=False):
            sq = sq_t[c % 2]
            op("act", lambda e: e.activation(out=sq[:], in_=src_ap_fn(c), func=AF.Square),
               reads=src_bufs_fn(c), writes=[Bsq[c % 2]])

            def pe_part():
                op("pe", lambda e: e.matmul(ps[7], lhsT=ones, rhs=sq[:], start=first, stop=last),
                   reads=[Bsq[c % 2], Bcst], writes=[Bps[7]])
            if defer:
                pending_stats.append(pe_part)
            else:
                pe_part()

        def finish_rstd():
            flush_stats()
            op("act", lambda e: e.activation(out=rs_t[:], in_=ps[7], func=AF.Sqrt, scale=1.0 / D, bias=EPS),
               reads=[Bps[7]], writes=[Brs])
            op("dve", lambda e: e.reciprocal(rstd_t[:], rs_t[:]), reads=[Brs], writes=[Brstd])

        def prenorm(vid, have_stats=False, reuse_rstd=False):
            if not reuse_rstd:
                if not have_stats:
                    for c in range(DC):
                        stats_from(lambda c: h_t[:, c, :], lambda c: [Bh[c]], c, c == 0, c == DC - 1)
                finish_rstd()
            for c in range(DC):
                op("dve", (lambda c: lambda e: e.scalar_tensor_tensor(out=u_t[:, c, :], in0=h_t[:, c, :], scalar=gcol(vid, c),
                                                                       in1=rstd_t[:], op0=ALU.mult, op1=ALU.mult))(c),
                   reads=[Bh[c], Bgv, Brstd], writes=[Bu[c]])

        def linear_fm(n_wtiles, rhs_fn, rhs_bufs_fn, evac):
            for wtile in range(n_wtiles):
                wt, bwt = next_w()
                for j in range(4):
                    b = next_bank()
                    for kc in range(DC):
                        op("pe", (lambda wt, j, kc, b: lambda e: e.matmul(ps[b], lhsT=wt[:, kc, j * 128:(j + 1) * 128], rhs=rhs_fn(kc),
                                                                          start=(kc == 0), stop=(kc == DC - 1)))(wt, j, kc, b),
                           reads=[bwt] + rhs_bufs_fn(kc), writes=[Bps[b]])
                    flush_stats()
                    evac(wtile * 4 + j, b)

        def evac_postnorm(vid):
            def f(c, b):
                op("act", lambda e: e.activation(out=m_t[:, c, :], in_=ps[b], func=AF.Identity, scale=gcol(vid, c)),
                   reads=[Bps[b], Bgv], writes=[Bm[c]])
                stats_from(lambda c_: ps[b], lambda c_: [Bps[b]], c, c == 0, c == DC - 1, defer=True)
            return f

        def residual_update(next_stats=True, after_chunk=None):
            finish_rstd()
            for c in range(DC):
                tmp = rt_t[c % 2]
                op("dve", (lambda c, tmp: lambda e: e.tensor_tensor(out=tmp[:], in0=m_t[:, c, :], in1=rstd_t[:], op=ALU.mult))(c, tmp),
                   reads=[Bm[c], Brstd], writes=[Brt[c % 2]])
                op("dve", (lambda c, tmp: lambda e: e.tensor_tensor(out=h_t[:, c, :], in0=h_t[:, c, :], in1=tmp[:], op=ALU.add))(c, tmp),
                   reads=[Bh[c], Brt[c % 2]], writes=[Bh[c]])
                if next_stats:
                    stats_from(lambda c: h_t[:, c, :], lambda c: [Bh[c]], c, c == 0, c == DC - 1)
                if after_chunk is not None:
                    after_chunk(c)

        def mlp(layer, next_stats=True, after_chunk=None):
            prenorm(G_MLPPRE0 + layer, have_stats=True)

            def evac_up(f, b):
                tmp = rt_t[f % 2]
                op("act", lambda e: e.activation(out=tmp[:], in_=ps[b], func=AF.Relu), reads=[Bps[b]], writes=[Brt[f % 2]])
                op("dve", lambda e: e.tensor_tensor(out=big_t[:, f, :], in0=tmp[:], in1=tmp[:], op=ALU.mult),
                   reads=[Brt[f % 2]], writes=[Bbig[f]])
            linear_fm(16, lambda kc: u_t[:, kc, :], lambda kc: [Bu[kc]], evac_up)
            ev = evac_postnorm(G_MLPPOST0 + layer)
            for nr in range(4):
                for q in range(4):
                    wt, bwt = next_w()
                    for j in range(4):
                        for kc in range(DC):
                            f = q * 16 + kc
                            op("pe", (lambda wt, j, kc, f, q: lambda e: e.matmul(ps[j], lhsT=wt[:, kc, j * 128:(j + 1) * 128], rhs=big_t[:, f, :],
                                                                                  start=(q == 0 and kc == 0), stop=(q == 3 and kc == DC - 1)))(wt, j, kc, f, q),
                               reads=[bwt, Bbig[f]], writes=[Bps[j]])
                        if q == 3:
                            flush_stats()
                            ev(nr * 4 + j, j)
            bank_rr["i"] = 0
            residual_update(next_stats=next_stats, after_chunk=after_chunk)

        for ti in range(NT):
            Rt = S - (ti + 1) * TT
            Lt = (ti + 1) * TT
            blk_lo = Rt // 128
            for g in range(4):
                op("sp", (lambda Rt, g: lambda e: e.dma_start(out=h_t[:, 4 * g:4 * g + 4, :],
                                                              in_=xin[4 * g:4 * g + 4, :, Rt:Rt + TT].rearrange("c p r -> p c r")))(Rt, g),
                   reads=[Bxin], writes=Bh[4 * g:4 * g + 4], chan=ch_xs[g])

            prenorm(G_MIXPRE0)

            def evac_in(cidx, b, ti=ti):
                grp, j = divmod(cidx, DC)
                if grp == 0:
                    op("act", lambda e: e.activation(out=m_t[:, j, :], in_=ps[b], func=AF.Identity),
                       reads=[Bps[b]], writes=[Bm[j]])
                elif grp == 1:
                    op("dve", lambda e: e.tensor_tensor(out=m_t[:, j, :], in0=ps[b], in1=m_t[:, j, :], op=ALU.mult),
                       reads=[Bps[b], Bm[j]], writes=[Bm[j]])
                else:
                    v = m_t[:, j, :]
                    op("dve", lambda e: e.tensor_scalar(out=cb_t[:], in0=v, scalar1=gcol(G_CW2, j), scalar2=None, op0=ALU.mult),
                       reads=[Bm[j], Bgv], writes=[Bcb])
                    op("dve", lambda e: e.scalar_tensor_tensor(out=cb_t[:, 0:TT - 1], in0=m_t[:, j, 1:TT], scalar=gcol(G_CW1, j),
                                                               in1=cb_t[:, 0:TT - 1], op0=ALU.mult, op1=ALU.add),
                       reads=[Bm[j], Bgv, Bcb], writes=[Bcb])
                    op("dve", lambda e: e.scalar_tensor_tensor(out=cb_t[:, 0:TT - 2], in0=m_t[:, j, 2:TT], scalar=gcol(G_CW0, j),
                                                               in1=cb_t[:, 0:TT - 2], op0=ALU.mult, op1=ALU.add),
                       reads=[Bm[j], Bgv, Bcb], writes=[Bcb])
                    if ti > 0:
                        op("dve", lambda e: e.scalar_tensor_tensor(out=cb_t[:, TT - 1:TT], in0=vc_t[:, j, 0:1], scalar=gcol(G_CW1, j),
                                                                   in1=cb_t[:, TT - 1:TT], op0=ALU.mult, op1=ALU.add),
                           reads=[Bvc, Bgv, Bcb], writes=[Bcb])
                        op("dve", lambda e: e.scalar_tensor_tensor(out=cb_t[:, TT - 2:TT], in0=vc_t[:, j, 0:2], scalar=gcol(G_CW0, j),
                                                                   in1=cb_t[:, TT - 2:TT], op0=ALU.mult, op1=ALU.add),
                           reads=[Bvc, Bgv, Bcb], writes=[Bcb])
                    op("dve", lambda e: e.tensor_copy(vc_t[:, j, :], m_t[:, j, 0:2]), reads=[Bm[j], Bcb], writes=[Bvc])
                    op("dve", lambda e: e.tensor_tensor(out=y_v[:, j, :], in0=ps[b], in1=cb_t[:], op=ALU.mult),
                       reads=[Bps[b], Bcb], writes=[Bbig[j]])
            linear_fm(12, lambda kc: u_t[:, kc, :], lambda kc: [Bu[kc]], evac_in)
            linear_fm(4, lambda kc: y_v[:, kc, :], lambda kc: [Bbig[kc]], evac_postnorm(G_MIXPOST0))
            residual_update()
            mlp(0)

            prenorm(G_KV, have_stats=True)

            def evac_k(c, b):
                op("act", lambda e: e.activation(out=Kst_v[:, c, :], in_=ps[b], func=AF.Identity),
                   reads=[Bps[b]], writes=[Bbig[c]])
            linear_fm(4, lambda kc: u_t[:, kc, :], lambda kc: [Bu[kc]], evac_k)
            op("pool", (lambda Rt: lambda e: e.dma_start(out=KT[:, :, Rt:Rt + TT].rearrange("h p r -> p h r"), in_=Kst_v))(Rt),
               reads=Bbig[0:16], writes=[Bkd[ti]], chan=ch_ks)
            if ti == 0:
                dbg("Kst", Kst_v, Bbig[0:16], [128, 16, 512], BF16)
            for g4 in range(4):
                wt, bwt = next_w()
                for tb in range(4):
                    b = next_bank()
                    for kc in range(DC):
                        op("pe", (lambda wt, kc, tb, b: lambda e: e.matmul(ps[b], lhsT=u_t[:, kc, tb * 128:(tb + 1) * 128], rhs=wt[:, kc, :],
                                                                           start=(kc == 0), stop=(kc == DC - 1)))(wt, kc, tb, b),
                           reads=[bwt, Bu[kc]], writes=[Bps[b]])
                    if tb % 2 == 0:
                        op("act", (lambda tb, g4, b: lambda e: e.activation(out=Vst_v[:, tb, g4 * 512:(g4 + 1) * 512], in_=ps[b], func=AF.Identity))(tb, g4, b),
                           reads=[Bps[b]], writes=Bbig[16 + 4 * tb + g4:16 + 4 * tb + g4 + 1])
                    else:
                        op("dve", (lambda tb, g4, b: lambda e: e.tensor_copy(Vst_v[:, tb, g4 * 512:(g4 + 1) * 512], ps[b]))(tb, g4, b),
                           reads=[Bps[b]], writes=Bbig[16 + 4 * tb + g4:16 + 4 * tb + g4 + 1])
            for tb in range(4):
                blk = blk_lo + tb
                op("pool", (lambda tb, blk: lambda e: e.dma_start(out=VS[:, :, blk, :].rearrange("h p d -> p h d"),
                                                                  in_=Vst_v[:, tb, :].rearrange("p (h d) -> p h d", d=128)))(tb, blk),
                   reads=Bbig[16 + 4 * tb:16 + 4 * tb + 4], writes=[Bvd[ti]], chan=ch_vs)

            prenorm(G_MIXPRE1, reuse_rstd=True)

            def evac_q(c, b):
                op("act", lambda e: e.activation(out=QT_v[:, c, :], in_=ps[b], func=AF.Identity),
                   reads=[Bps[b]], writes=[Bbig[c]])
            linear_fm(4, lambda kc: u_t[:, kc, :], lambda kc: [Bu[kc]], evac_q)

            nvb = Lt // 128
            if ti == 0:
                dbg("QT", QT_v, Bbig[0:16], [128, 16, 512], BF16)
                dbg("h1", h_t[:], Bh, [128, 16, 512], F32)

            def load_head(hh, slot, Rt=Rt, Lt=Lt, nvb=nvb, blk_lo=blk_lo, ti=ti):
                op("sp", lambda e: e.dma_start(out=KTh_v[slot][:, 0:Lt], in_=KT[hh, :, Rt:S]),
                   reads=Bkd[0:ti + 1], writes=BKTh[slot], chan=ch_kl[slot])
                op("sp", lambda e: e.dma_start(out=Vh_v[slot][:, 0:nvb, :], in_=VS[hh, :, blk_lo:NBLK, :]),
                   reads=Bvd[0:ti + 1], writes=BVh[slot], chan=ch_vl[slot])

            units = []
            for hh in range(NH):
                for qb in range(4):
                    k0 = 128 * qb
                    L = Lt - k0
                    npieces = (L + 511) // 512
                    for g0p in range(0, npieces, 2):
                        gp = min(2, npieces - g0p)
                        g0 = g0p * 512
                        gn = min(L - g0, 1024)
                        units.append(dict(hh=hh, slot=hh % 2, qb=qb, k0=k0, L=L, g0=g0, gp=gp, gn=gn,
                                          first=(g0p == 0), last=(g0 + gn >= L), zb=2 * (len(units) % 2), wo=1024 * (len(units) % 4)))
            tslot = [0]

            def stageA(u):
                hh, slot, k0, L, g0, gp, gn = u["hh"], u["slot"], u["k0"], u["L"], u["g0"], u["gp"], u["gn"]
                zb = u["zb"]
                zflat = ps_all[:, zb:zb + 2, :].rearrange("p a b -> p (a b)")
                qT = QT_v[:, hh, k0:k0 + 128]
                kth = KTh_v[slot]
                for pi in range(gp):
                    c0 = g0 + pi * 512
                    n = min(512, L - c0)
                    diag = (c0 == 0)
                    op("pe", (lambda pi, c0, n, diag: lambda e: e.matmul(ps[zb + pi][:, 0:n], lhsT=qT, rhs=kth[:, k0 + c0:k0 + c0 + n],
                                                                         start=True, stop=not diag))(pi, c0, n, diag),
                       reads=[Bbig[hh]] + BKTh[slot], writes=[Bps[zb + pi]])
                    if diag:
                        op("pe", lambda e: e.matmul(ps[zb][:, 0:128], lhsT=ident, rhs=maskm, start=False, stop=True),
                           reads=[Bcst], writes=[Bps[zb]])
                wo = u["wo"]
                op("act", lambda e: e.activation(out=uflat32[:, wo:wo + gn], in_=zflat[:, 0:gn], func=AF.Sigmoid, scale=-SCALE),
                   reads=Bps[zb:zb + gp], writes=bw(wo, wo + gn))

            def stageB(u):
                g0, gn = u["g0"], u["gn"]
                if u["first"]:
                    op("dve", lambda e: e.tensor_copy(P_v[:, 0:1], one_t[:]), reads=[Bone], writes=bP(0, 1))
                wo = u["wo"]
                op("dve", lambda e: e.tensor_tensor_scan(out=P_v[:, 1 + g0:1 + g0 + gn], data0=uflat32[:, wo:wo + gn],
                                                         data1=uflat32[:, wo:wo + gn], initial=P_v[:, g0:g0 + 1],
                                                         op0=ALU.mult, op1=ALU.min),
                   reads=bw(wo, wo + gn) + bP(g0, g0 + 1), writes=bP(1 + g0, 1 + g0 + gn))
                op("dve", lambda e: e.tensor_tensor(out=a_v[:, g0:g0 + gn], in0=P_v[:, g0:g0 + gn], in1=P_v[:, g0 + 1:g0 + gn + 1],
                                                    op=ALU.subtract),
                   reads=bP(g0, g0 + gn + 1), writes=ba(g0, g0 + gn))

            def stageC(u):
                hh, slot, qb, k0, L, g0, gn = u["hh"], u["slot"], u["qb"], u["k0"], u["L"], u["g0"], u["gn"]
                vh = Vh_v[slot]
                nkb = gn // 128
                for t0 in range(0, nkb, 8):
                    tn = min(8, nkb - t0)
                    tb_ = 4 + (tslot[0] % 2)
                    at = aT_v[tslot[0] % 2]
                    bat = BaT[tslot[0] % 2]
                    tslot[0] += 1
                    ptv = ps[tb_].bitcast(BF16).rearrange("p (a b) -> p a b", b=128)
                    for i in range(tn):
                        cc = g0 + (t0 + i) * 128
                        op("pe", (lambda i, cc, ptv: lambda e: e.transpose(ptv[:, i, :], a_v[:, cc:cc + 128], ident))(i, cc, ptv),
                           reads=ba(cc, cc + 128) + [Bcst], writes=[Bps[tb_]])
                    op("act", (lambda at, ptv, tn: lambda e: e.activation(out=at[:, 0:tn, :], in_=ptv[:, 0:tn, :], func=AF.Identity))(at, ptv, tn),
                       reads=[Bps[tb_]], writes=bat)
                    for i in range(tn):
                        kb = t0 + i
                        vb = (k0 + g0) // 128 + kb
                        fa = (u["first"] and kb == 0)
                        la = (g0 + (kb + 1) * 128 >= L)
                        op("pe", (lambda i, vb, fa, la, at: lambda e: e.matmul(ps[6][:, k0:k0 + 128], lhsT=vh[:, vb, :], rhs=at[:, i, :],
                                                                               start=fa, stop=la))(i, vb, fa, la, at),
                           reads=BVh[slot] + bat, writes=[Bps[6]])
                if u["last"] and qb == 3:
                    op("act", lambda e: e.activation(out=OT_v[:, hh, :], in_=ps[6], func=AF.Identity),
                       reads=[Bps[6]], writes=[Bbig[16 + hh]])

            load_head(0, 0)
            stageA(units[0])
            stageA(units[1])
            for i, u in enumerate(units):
                if u["qb"] == 0 and u["first"] and u["hh"] + 1 < NH:
                    load_head(u["hh"] + 1, (u["hh"] + 1) % 2)
                if i + 2 < len(units):
                    stageA(units[i + 2])
                stageB(u)
                if ti == 0 and i == 0:
                    dbg("w0", uflat32[:, 0:512], bw(0, 512), [128, 512], F32)
                    dbg("P0", P_v[:, 0:513], bP(0, 513), [128, 513], F32)
                    dbg("a0", a_v[:, 0:512], ba(0, 512), [128, 512], BF16)
                    dbg("K0", KTh_v[0][:, 0:512], BKTh[0], [128, 512], BF16)
                    dbg("V0", Vh_v[0][:, 0:4, :], BVh[0], [128, 4, 128], BF16)
                stageC(u)
                if ti == 0 and i == 0:
                    dbg("aT0", aT_v[0], BaT[0], [128, 8, 128], BF16)
            if ti == 0:
                dbg("OT", OT_v, BOT, [128, 16, 512], BF16)
            bank_rr["i"] = 0
            linear_fm(4, lambda kc: OT_v[:, kc, :], lambda kc: [Bbig[16 + kc]], evac_postnorm(G_MIXPOST1))
            residual_update()
            def store_chunks(c, Rt=Rt, ti=ti):
                if c % 4 == 3:
                    g = c // 4
                    op("pool", lambda e: e.dma_start(out=outT[4 * g:4 * g + 4, :, Rt:Rt + TT].rearrange("c p r -> p c r"),
                                                     in_=h_t[:, 4 * g:4 * g + 4, :]),
                       reads=Bh[4 * g:4 * g + 4], writes=[Bout[ti]], chan=ch_os[g])
            mlp(1, next_stats=False, after_chunk=store_chunks)
        S_.wait_events("pool", [("c", c, c.count) for c in ch_os] + [("c", c, c.count) for c in dbg_chans])
        S_.emit()
    return nc


def _host_consts():
    c = np.zeros((128, 384), np.float32)
    c[:, 0:128] = np.eye(128)
    c[:, 128:256] = 1.0
    rt = np.arange(128)[:, None]
    rs = np.arange(128)[None, :]
    c[:, 256:384] = np.where(rs <= rt, NEG, 0.0)
    return c.astype(ml_dtypes.bfloat16)


def _gvec(mix_pre_g, mix_post_g, mlp_pre_g, mlp_post_g, kv_norm_g, a_conv_w):
    vecs = [mix_pre_g[0], mix_pre_g[1], mix_post_g[0], mix_post_g[1], mlp_pre_g[0], mlp_pre_g[1],
            mlp_post_g[0], mlp_post_g[1], kv_norm_g, a_conv_w[0, 0], a_conv_w[0, 1], a_conv_w[0, 2]]
    g = np.stack([np.asarray(v, np.float32).reshape(DC, 128).T for v in vecs], axis=1)
    return np.ascontiguousarray(g.reshape(128, NVEC * DC))


_NC_CACHE = {}


def run_cores(x, a_w_in, a_conv_w, a_w_out, kv_norm_g, w_kv, b_w_q, b_w_o,
              mix_pre_g, mix_post_g, mlp_pre_g, mlp_post_g, mlp_w_up, mlp_w_down):
    x = np.asarray(x, np.float32)
    B, S, _ = x.shape
    NT = S // TT
    if NT not in _NC_CACHE:
        _NC_CACHE[NT] = build_nc(NT)
    nc = _NC_CACHE[NT]
    shared = {
        "gvec": _gvec(*(np.asarray(v, np.float32) for v in (mix_pre_g, mix_post_g, mlp_pre_g, mlp_post_g, kv_norm_g, a_conv_w))),
        "consts": _host_consts(),
        "w_in": np.ascontiguousarray(np.asarray(a_w_in, np.float32)[0]),
        "w_out": np.ascontiguousarray(np.asarray(a_w_out, np.float32)[0]),
        "w_kv": np.ascontiguousarray(np.asarray(w_kv, np.float32)),
        "w_q": np.ascontiguousarray(np.asarray(b_w_q, np.float32)[0]),
        "w_o": np.ascontiguousarray(np.asarray(b_w_o, np.float32)[0]),
        "w_up": np.ascontiguousarray(np.asarray(mlp_w_up, np.float32)),
        "w_dn": np.ascontiguousarray(np.asarray(mlp_w_down, np.float32)),
    }
    in_maps = []
    for b in range(B):
        xr = np.ascontiguousarray(x[b, ::-1, :].T).reshape(DC, 128, S)
        m = dict(shared)
        m["xin"] = xr
        in_maps.append(m)
    res = run_bass_kernel_spmd(nc, in_maps, core_ids=list(range(B)))
    global LAST_RES
    LAST_RES = res.results
    out = np.empty((B, S, D), np.float32)
    for b in range(B):
        o = res.results[b]["outT"].reshape(D, S)
        out[b] = o.T[::-1, :]
    return out


def kernel(**inputs):
    return run_cores(**inputs)
```

```python
import os
import numpy as np
import ml_dtypes
from contextlib import ExitStack
import concourse.bass as bass
import concourse.mybir as mybir
from concourse.bass_utils import run_bass_kernel_spmd

F32 = mybir.dt.float32
BF16 = mybir.dt.bfloat16
ALU = mybir.AluOpType
AF = mybir.ActivationFunctionType

D = 2048
DC = 16
TT = 512
DFF = 8192
FC = 64
NH = 16
EPS = 1e-6
NEG = -4096.0
SCALE = 128 ** -0.5

EPOCH = 16000
ENGS = ("pe", "act", "dve", "pool", "sp")


class Buf:
    __slots__ = ("name", "w", "r", "excl")

    def __init__(self, name, excl=False):
        self.name = name
        self.excl = excl
        self.w = None
        self.r = []


class Chan:
    __slots__ = ("sem", "count", "name")

    def __init__(self, sem, name=""):
        self.sem = sem
        self.count = 0
        self.name = name


class Ins:
    __slots__ = ("eng", "fn", "waits", "signal", "chan", "sigidx")

    def __init__(self, eng, fn, chan):
        self.eng = eng
        self.fn = fn
        self.waits = []
        self.signal = False
        self.chan = chan
        self.sigidx = 0


class Sched:
    def __init__(self, nc, stack, n_epochs=6):
        self.nc = nc
        self.stack = stack
        self.q = {e: [] for e in ENGS}
        self.seen_e = {e: {} for e in ENGS}
        self.seen_c = {e: {} for e in ENGS}
        self.sems = {e: [stack.enter_context(nc.semaphore(f"s_{e}{k}")) for k in range(n_epochs)]
                     for e in ("pe", "act", "dve")}
        self.n_epochs = n_epochs
        self.all_chans = []

    def chan(self, name):
        c = Chan(self.stack.enter_context(self.nc.semaphore("c_" + name)), name)
        self.all_chans.append(c)
        return c

    def op(self, eng, fn, reads=(), writes=(), chan=None):
        q = self.q[eng]
        idx = len(q)
        ins = Ins(eng, fn, chan)
        is_dma = chan is not None
        need_e = {}
        need_c = {}
        if is_dma:
            if chan.count > 0:
                need_c[chan] = chan.count
            chan.count += 16
            ev = ("c", chan, chan.count)
        else:
            ev = ("e", eng, idx)

        def add(dep, raw):
            if dep is None:
                return
            if dep[0] == "e":
                _, e2, i2 = dep
                if e2 == eng and not is_dma:
                    if not raw or eng == "pe":
                        return
                if need_e.get(e2, -1) < i2:
                    need_e[e2] = i2
            else:
                _, c2, v2 = dep
                if need_c.get(c2, -1) < v2:
                    need_c[c2] = v2

        if any(b.excl for b in reads):
            writes = list(writes) + [b for b in reads if b.excl and b not in writes]
        for b in reads:
            add(b.w, True)
        for b in writes:
            add(b.w, False)
            for d in b.r:
                add(d, False)
        se = self.seen_e[eng]
        for e2, i2 in need_e.items():
            if se.get(e2, -1) >= i2:
                continue
            se[e2] = i2
            self.q[e2][i2].signal = True
            ins.waits.append(("e", e2, i2))
        sc = self.seen_c[eng]
        for c2, v2 in need_c.items():
            if sc.get(c2, -1) >= v2:
                continue
            sc[c2] = v2
            ins.waits.append(("c", c2, v2))
        for b in reads:
            if ev[0] == "e":
                b.r = [d for d in b.r if not (d[0] == "e" and d[1] == eng)]
            else:
                b.r = [d for d in b.r if not (d[0] == "c" and d[1] is chan)]
            b.r.append(ev)
        for b in writes:
            b.w = ev
            b.r = []
        q.append(ins)
        return ev

    def wait_events(self, eng, events):
        ins = Ins(eng, None, None)
        for ev in events:
            if ev[0] == "e":
                self.q[ev[1]][ev[2]].signal = True
            ins.waits.append(ev)
        self.q[eng].append(ins)

    def emit(self):
        nc = self.nc
        for e in ENGS:
            c = 0
            for ins in self.q[e]:
                if ins.signal and ins.chan is None and ins.fn is not None:
                    assert e in self.sems, e
                    c += 1
                    ins.sigidx = c
            assert c <= EPOCH * self.n_epochs, (e, c)
        sems = self.sems
        qs = self.q
        for e in sems:
            for sh in sems[e]:
                nc.gpsimd.sem_clear(sh)
        for c in self.all_chans:
            nc.gpsimd.sem_clear(c.sem)
        nc.all_engine_barrier()

        def run(e, engine):
            for ins in qs[e]:
                for w in ins.waits:
                    if w[0] == "e":
                        s = qs[w[1]][w[2]].sigidx
                        assert s > 0
                        engine.wait_ge(sems[w[1]][(s - 1) // EPOCH], (s - 1) % EPOCH + 1)
                    else:
                        engine.wait_ge(w[1].sem, w[2])
                if ins.fn is None:
                    continue
                bi = ins.fn(engine)
                if ins.chan is not None:
                    bi.then_inc(ins.chan.sem, 16)
                elif ins.signal:
                    s = ins.sigidx
                    bi.then_inc(sems[e][(s - 1) // EPOCH], 1)

        with nc.Block() as block:
            @block.sync
            def _(eng):
                run("sp", eng)

            @block.tensor
            def _(eng):
                run("pe", eng)

            @block.scalar
            def _(eng):
                run("act", eng)

            @block.vector
            def _(eng):
                run("dve", eng)

            @block.gpsimd
            def _(eng):
                run("pool", eng)


G_MIXPRE0, G_MIXPRE1, G_MIXPOST0, G_MIXPOST1, G_MLPPRE0, G_MLPPRE1, G_MLPPOST0, G_MLPPOST1, G_KV, G_CW0, G_CW1, G_CW2 = range(12)
NVEC = 12


def build_nc(NT):
    S = NT * TT
    NBLK = S // 128
    nc = bass.Bass("TRN2", target_bir_lowering=False)
    dt_in = lambda name, shape, dt=F32: nc.dram_tensor(name, shape, dt, kind="ExternalInput").ap()
    xin = dt_in("xin", [DC, 128, S])
    gvec = dt_in("gvec", [128, NVEC * DC])
    consts = dt_in("consts", [128, 384], BF16)
    w_in = dt_in("w_in", [D, 3 * D])
    w_out = dt_in("w_out", [D, D])
    w_kv = dt_in("w_kv", [D, 2 * D])
    w_q = dt_in("w_q", [D, D])
    w_o = dt_in("w_o", [D, D])
    w_up = dt_in("w_up", [2, D, DFF])
    w_dn = dt_in("w_dn", [2, DFF, D])
    outT = nc.dram_tensor("outT", [DC, 128, S], F32, kind="ExternalOutput").ap()
    WS = nc.dram_tensor("WS", [96, 128, DC * 512], BF16, kind="Internal").ap()
    _kvkind = "ExternalOutput" if os.environ.get("KDBG", "") != "" else "Internal"
    KT = nc.dram_tensor("dbg_KT" if _kvkind != "Internal" else "KT", [NH, 128, S], BF16, kind=_kvkind).ap()
    VS = nc.dram_tensor("dbg_VS" if _kvkind != "Internal" else "VS", [NH, 128, NBLK, 128], BF16, kind=_kvkind).ap()

    wsrc = []
    for k in range(12):
        wsrc.append(w_in[:, k * 512:(k + 1) * 512])
    for k in range(4):
        wsrc.append(w_out[:, k * 512:(k + 1) * 512])
    for k in range(16):
        wsrc.append(w_up[0, :, k * 512:(k + 1) * 512])
    for nr in range(4):
        for q in range(4):
            wsrc.append(w_dn[0, q * 2048:(q + 1) * 2048, nr * 512:(nr + 1) * 512])
    for k in range(8):
        wsrc.append(w_kv[:, k * 512:(k + 1) * 512])
    for k in range(4):
        wsrc.append(w_q[:, k * 512:(k + 1) * 512])
    for k in range(4):
        wsrc.append(w_o[:, k * 512:(k + 1) * 512])
    for k in range(16):
        wsrc.append(w_up[1, :, k * 512:(k + 1) * 512])
    for nr in range(4):
        for q in range(4):
            wsrc.append(w_dn[1, q * 2048:(q + 1) * 2048, nr * 512:(nr + 1) * 512])
    assert len(wsrc) == 96

    with ExitStack() as st:
        S_ = Sched(nc, st)
        op = S_.op
        sb = lambda name, shape, dt: st.enter_context(nc.sbuf_tensor(name, shape, dt))
        h_t = sb("h_t", [128, DC, TT], F32)
        u_t = sb("u_t", [128, DC, TT], BF16)
        m_t = sb("m_t", [128, DC, TT], F32)
        big_t = sb("big_t", [128, FC, TT], BF16)
        NWB = 3
        wt_t = [sb(f"wt{i}", [128, DC, 512], BF16) for i in range(NWB)]
        gv_t = sb("gv_t", [128, NVEC * DC], F32)
        cst_t = sb("cst_t", [128, 384], BF16)
        sq_t = [sb(f"sq{i}", [128, TT], BF16) for i in range(2)]
        rt_t = [sb(f"rt{i}", [128, TT], F32) for i in range(2)]
        rs_t = sb("rs_t", [128, TT], F32)
        rstd_t = sb("rstd_t", [128, TT], F32)
        cb_t = sb("cb_t", [128, TT], F32)
        vc_t = sb("vc_t", [128, DC, 2], F32)
        one_t = sb("one_t", [128, 1], F32)
        ps_all = st.enter_context(nc.psum_tensor("ps_all", [128, 8, 512], F32))
        ps = [ps_all[:, i, :] for i in range(8)]
        pzflat = ps_all[:, 0:4, :].rearrange("p a b -> p (a b)")
        ident = cst_t[:, 0:128]
        ones = cst_t[:, 128:256]
        maskm = cst_t[:, 256:384]

        Bh = [Buf(f"h{c}") for c in range(DC)]
        Bu = [Buf(f"u{c}") for c in range(DC)]
        Bm = [Buf(f"m{c}") for c in range(DC)]
        Bbig = [Buf(f"big{c}") for c in range(FC)]
        Bwt = [Buf(f"wt{i}") for i in range(NWB)]
        Bps = [Buf(f"ps{i}", excl=True) for i in range(8)]
        Bgv, Bcst = Buf("gv"), Buf("cst")
        Bsq = [Buf("sq0"), Buf("sq1")]
        Brt = [Buf("rt0"), Buf("rt1")]
        Brs, Brstd, Bcb, Bvc, Bone = Buf("rs"), Buf("rstd"), Buf("cb"), Buf("vc"), Buf("one")
        Bws = [Buf(f"ws{k}") for k in range(96)]
        Bkd = [Buf(f"kd{i}") for i in range(NT)]
        Bvd = [Buf(f"vd{i}") for i in range(NT)]
        Bxin = Buf("xin_dram")
        Bout = [Buf(f"out{i}") for i in range(NT)]

        ch_w = [S_.chan(f"w{i}") for i in range(NWB)]
        ch_cast = [S_.chan(f"cast{i}") for i in range(6)]
        ch_g, ch_c = S_.chan("g"), S_.chan("c")
        ch_xs = [S_.chan(f"x{i}") for i in range(4)]
        ch_os = [S_.chan(f"o{i}") for i in range(4)]
        ch_ks, ch_vs = S_.chan("ks"), S_.chan("vs")
        ch_kl = [S_.chan("kl0"), S_.chan("kl1")]
        ch_vl = [S_.chan("vl0"), S_.chan("vl1")]

        DBG = os.environ.get("KDBG", "") != ""
        dbg_n = [0]

        def dbg(name, ap, bufs, shape, dt):
            if not DBG:
                return
            t = nc.dram_tensor("dbg_" + name, list(shape), dt, kind="ExternalOutput").ap()
            ch = S_.chan(f"dbg{dbg_n[0]}")
            dbg_n[0] += 1
            op("pool", lambda e: e.dma_start(out=t, in_=ap), reads=bufs, writes=[Buf("dbgd")], chan=ch)
            dbg_chans.append(ch)
        dbg_chans = []

        mflat = m_t[:].rearrange("p c n -> p (c n)")
        uflat32 = u_t[:].rearrange("p c n -> p (c n)").bitcast(F32)
        P_r = [mflat[:, 1536 * k:1536 * k + 1025] for k in range(3)]
        a_r = [mflat[:, 4608 + 512 * k:4608 + 512 * (k + 1)].bitcast(BF16) for k in range(3)]
        aT_v = [mflat[:, 6144 + 512 * i: 6144 + 512 * (i + 1)].bitcast(BF16).rearrange("p (a b) -> p a b", b=128)
                for i in range(2)]
        BP_r = [Bm[3 * k:3 * k + 3] for k in range(3)]
        Ba_r = [[Bm[9 + k]] for k in range(3)]
        bigflat = big_t[:].rearrange("p c n -> p (c n)")
        y_v = big_t[:, 0:16, :]
        QT_v = big_t[:, 0:16, :]
        OT_v = big_t[:, 16:32, :]
        Kst_v = big_t[:, 0:16, :]
        Vst_v = bigflat[:, 16 * 512:32 * 512].rearrange("p (t n) -> p t n", n=2048)
        KTh_v = [bigflat[:, (32 + 8 * i) * 512:(40 + 8 * i) * 512] for i in range(2)]
        Vh_v = [bigflat[:, (48 + 8 * i) * 512:(56 + 8 * i) * 512].rearrange("p (b d) -> p b d", d=128) for i in range(2)]

        def bw(a, b):
            return Bu[(4 * a) // 1024:(4 * b - 1) // 1024 + 1]

        def bP(a, b):
            return Bm[(4 * a) // 2048:(4 * b - 1) // 2048 + 1]

        def ba(a, b):
            return Bm[9 + (2 * a) // 2048: 9 + (2 * b - 1) // 2048 + 1]

        BaT = [[Bm[12]], [Bm[13]]]
        BQT = Bbig[0:16]
        BOT = Bbig[16:32]
        BKTh = [Bbig[32:40], Bbig[40:48]]
        BVh = [Bbig[48:56], Bbig[56:64]]

        def gcol(vid, c):
            return gv_t[:, vid * DC + c: vid * DC + c + 1]

        op("sp", lambda e: e.dma_start(out=gv_t[:], in_=gvec), writes=[Bgv], chan=ch_g)
        op("sp", lambda e: e.dma_start(out=cst_t[:], in_=consts), writes=[Bcst], chan=ch_c)
        op("dve", lambda e: e.memset(one_t[:], 1.0), writes=[Bone])
        op("dve", lambda e: e.memset(vc_t[:].rearrange("p c n -> p (c n)"), 0.0), writes=[Bvc])

        wstate = {"cast": 0, "load": 0, "use": 0}
        total_w = 96 * NT

        def emit_cast_upto(k_lim):
            while wstate["cast"] < min(k_lim, 96):
                k = wstate["cast"]
                src = wsrc[k].rearrange("(c p) n -> p c n", p=128)
                dst = WS[k].rearrange("p (c n) -> p c n", n=512)
                op("pool", (lambda src, dst: lambda e: e.dma_start(out=dst, in_=src))(src, dst),
                   writes=[Bws[k]], chan=ch_cast[k % len(ch_cast)])
                wstate["cast"] += 1

        def emit_loads_upto(g_lim):
            while wstate["load"] < min(g_lim, total_w):
                g = wstate["load"]
                k = g % 96
                if g < 96:
                    emit_cast_upto(k + 5)
                bi = g % NWB
                op("sp", (lambda bi, k: lambda e: e.dma_start(out=wt_t[bi][:].rearrange("p c n -> p (c n)"), in_=WS[k]))(bi, k),
                   reads=[Bws[k]], writes=[Bwt[bi]], chan=ch_w[bi])
                wstate["load"] += 1

        def next_w():
            g = wstate["use"]
            emit_loads_upto(g + NWB)
            wstate["use"] += 1
            return wt_t[g % NWB], Bwt[g % NWB]

        bank_rr = {"i": 0}

        def next_bank():
            b = bank_rr["i"] % 4
            bank_rr["i"] += 1
            return b

        pending_stats = []

        def flush_stats():
            while pending_stats:
                pending_stats.pop(0)()

        def stats_from(src_ap_fn, src_bufs_fn, c, first, last, defer=False):
            sq = sq_t[c % 2]
            op("act", lambda e: e.activation(out=sq[:], in_=src_ap_fn(c), func=AF.Square),
               reads=src_bufs_fn(c), writes=[Bsq[c % 2]])

            def pe_part():
                op("pe", lambda e: e.matmul(ps[7], lhsT=ones, rhs=sq[:], start=first, stop=last),
                   reads=[Bsq[c % 2], Bcst], writes=[Bps[7]])
            if defer:
                pending_stats.append(pe_part)
            else:
                pe_part()

        def finish_rstd():
            flush_stats()
            op("act", lambda e: e.activation(out=rs_t[:], in_=ps[7], func=AF.Sqrt, scale=1.0 / D, bias=EPS),
               reads=[Bps[7]], writes=[Brs])
            op("dve", lambda e: e.reciprocal(rstd_t[:], rs_t[:]), reads=[Brs], writes=[Brstd])

        def prenorm(vid, have_stats=False, reuse_rstd=False):
            if not reuse_rstd:
                if not have_stats:
                    for c in range(DC):
                        stats_from(lambda c: h_t[:, c, :], lambda c: [Bh[c]], c, c == 0, c == DC - 1)
                finish_rstd()
            for c in range(DC):
                op("dve", (lambda c: lambda e: e.scalar_tensor_tensor(out=u_t[:, c, :], in0=h_t[:, c, :], scalar=gcol(vid, c),
                                                                       in1=rstd_t[:], op0=ALU.mult, op1=ALU.mult))(c),
                   reads=[Bh[c], Bgv, Brstd], writes=[Bu[c]])

        def linear_fm(n_wtiles, rhs_fn, rhs_bufs_fn, evac):
            for wtile in range(n_wtiles):
                wt, bwt = next_w()
                if wtile == 0:
                    bs = [next_bank() for _ in range(4)]
                    for kc in range(DC):
                        for j in range(4):
                            b = bs[j]
                            op("pe", (lambda wt, j, kc, b: lambda e: e.matmul(ps[b], lhsT=wt[:, kc, j * 128:(j + 1) * 128], rhs=rhs_fn(kc),
                                                                              start=(kc == 0), stop=(kc == DC - 1)))(wt, j, kc, b),
                               reads=[bwt] + rhs_bufs_fn(kc), writes=[Bps[b]])
                    for j in range(4):
                        flush_stats()
                        evac(j, bs[j])
                    continue
                for j in range(4):
                    b = next_bank()
                    for kc in range(DC):
                        op("pe", (lambda wt, j, kc, b: lambda e: e.matmul(ps[b], lhsT=wt[:, kc, j * 128:(j + 1) * 128], rhs=rhs_fn(kc),
                                                                          start=(kc == 0), stop=(kc == DC - 1)))(wt, j, kc, b),
                           reads=[bwt] + rhs_bufs_fn(kc), writes=[Bps[b]])
                    flush_stats()
                    evac(wtile * 4 + j, b)

        def evac_postnorm(vid):
            def f(c, b):
                op("act", lambda e: e.activation(out=m_t[:, c, :], in_=ps[b], func=AF.Identity, scale=gcol(vid, c)),
                   reads=[Bps[b], Bgv], writes=[Bm[c]])
                stats_from(lambda c_: ps[b], lambda c_: [Bps[b]], c, c == 0, c == DC - 1, defer=True)
            return f

        def residual_update(next_stats=True, after_chunk=None):
            finish_rstd()

            def tmul(c):
                tmp = rt_t[c % 2]
                op("dve", lambda e: e.tensor_tensor(out=tmp[:], in0=m_t[:, c, :], in1=rstd_t[:], op=ALU.mult),
                   reads=[Bm[c], Brstd], writes=[Brt[c % 2]])
            tmul(0)
            for c in range(DC):
                if c + 1 < DC:
                    tmul(c + 1)
                tmp = rt_t[c % 2]
                op("dve", (lambda c, tmp: lambda e: e.tensor_tensor(out=h_t[:, c, :], in0=h_t[:, c, :], in1=tmp[:], op=ALU.add))(c, tmp),
                   reads=[Bh[c], Brt[c % 2]], writes=[Bh[c]])
                if next_stats:
                    stats_from(lambda c: h_t[:, c, :], lambda c: [Bh[c]], c, c == 0, c == DC - 1)
                if after_chunk is not None:
                    after_chunk(c)

        def mlp(layer, next_stats=True, after_chunk=None):
            prenorm(G_MLPPRE0 + layer, have_stats=True)

            def evac_up(f, b):
                tmp = rt_t[f % 2]
                op("act", lambda e: e.activation(out=tmp[:], in_=ps[b], func=AF.Relu), reads=[Bps[b]], writes=[Brt[f % 2]])
                op("dve", lambda e: e.tensor_tensor(out=big_t[:, f, :], in0=tmp[:], in1=tmp[:], op=ALU.mult),
                   reads=[Brt[f % 2]], writes=[Bbig[f]])
            linear_fm(16, lambda kc: u_t[:, kc, :], lambda kc: [Bu[kc]], evac_up)
            ev = evac_postnorm(G_MLPPOST0 + layer)
            for nr in range(4):
                for q in range(4):
                    wt, bwt = next_w()
                    for j in range(4):
                        for kc in range(DC):
                            f = q * 16 + kc
                            op("pe", (lambda wt, j, kc, f, q: lambda e: e.matmul(ps[j], lhsT=wt[:, kc, j * 128:(j + 1) * 128], rhs=big_t[:, f, :],
                                                                                  start=(q == 0 and kc == 0), stop=(q == 3 and kc == DC - 1)))(wt, j, kc, f, q),
                               reads=[bwt, Bbig[f]], writes=[Bps[j]])
                        if q == 3:
                            flush_stats()
                            ev(nr * 4 + j, j)
            bank_rr["i"] = 0
            residual_update(next_stats=next_stats, after_chunk=after_chunk)

        for ti in range(NT):
            Rt = S - (ti + 1) * TT
            Lt = (ti + 1) * TT
            blk_lo = Rt // 128
            for g in range(4):
                op("sp", (lambda Rt, g: lambda e: e.dma_start(out=h_t[:, 4 * g:4 * g + 4, :],
                                                              in_=xin[4 * g:4 * g + 4, :, Rt:Rt + TT].rearrange("c p r -> p c r")))(Rt, g),
                   reads=[Bxin], writes=Bh[4 * g:4 * g + 4], chan=ch_xs[g])

            prenorm(G_MIXPRE0)

            def evac_in(cidx, b, ti=ti):
                grp, j = divmod(cidx, DC)
                if grp == 0:
                    op("act", lambda e: e.activation(out=m_t[:, j, :], in_=ps[b], func=AF.Identity),
                       reads=[Bps[b]], writes=[Bm[j]])
                elif grp == 1:
                    op("dve", lambda e: e.tensor_tensor(out=m_t[:, j, :], in0=ps[b], in1=m_t[:, j, :], op=ALU.mult),
                       reads=[Bps[b], Bm[j]], writes=[Bm[j]])
                else:
                    v = m_t[:, j, :]
                    op("dve", lambda e: e.tensor_scalar(out=cb_t[:], in0=v, scalar1=gcol(G_CW2, j), scalar2=None, op0=ALU.mult),
                       reads=[Bm[j], Bgv], writes=[Bcb])
                    op("dve", lambda e: e.scalar_tensor_tensor(out=cb_t[:, 0:TT - 1], in0=m_t[:, j, 1:TT], scalar=gcol(G_CW1, j),
                                                               in1=cb_t[:, 0:TT - 1], op0=ALU.mult, op1=ALU.add),
                       reads=[Bm[j], Bgv, Bcb], writes=[Bcb])
                    op("dve", lambda e: e.scalar_tensor_tensor(out=cb_t[:, 0:TT - 2], in0=m_t[:, j, 2:TT], scalar=gcol(G_CW0, j),
                                                               in1=cb_t[:, 0:TT - 2], op0=ALU.mult, op1=ALU.add),
                       reads=[Bm[j], Bgv, Bcb], writes=[Bcb])
                    if ti > 0:
                        op("dve", lambda e: e.scalar_tensor_tensor(out=cb_t[:, TT - 1:TT], in0=vc_t[:, j, 0:1], scalar=gcol(G_CW1, j),
                                                                   in1=cb_t[:, TT - 1:TT], op0=ALU.mult, op1=ALU.add),
                           reads=[Bvc, Bgv, Bcb], writes=[Bcb])
                        op("dve", lambda e: e.scalar_tensor_tensor(out=cb_t[:, TT - 2:TT], in0=vc_t[:, j, 0:2], scalar=gcol(G_CW0, j),
                                                                   in1=cb_t[:, TT - 2:TT], op0=ALU.mult, op1=ALU.add),
                           reads=[Bvc, Bgv, Bcb], writes=[Bcb])
                    op("dve", lambda e: e.tensor_copy(vc_t[:, j, :], m_t[:, j, 0:2]), reads=[Bm[j], Bcb], writes=[Bvc])
                    op("dve", lambda e: e.tensor_tensor(out=y_v[:, j, :], in0=ps[b], in1=cb_t[:], op=ALU.mult),
                       reads=[Bps[b], Bcb], writes=[Bbig[j]])
            linear_fm(12, lambda kc: u_t[:, kc, :], lambda kc: [Bu[kc]], evac_in)
            linear_fm(4, lambda kc: y_v[:, kc, :], lambda kc: [Bbig[kc]], evac_postnorm(G_MIXPOST0))
            residual_update()
            mlp(0)

            prenorm(G_KV, have_stats=True)

            def evac_k(c, b):
                op("act", lambda e: e.activation(out=Kst_v[:, c, :], in_=ps[b], func=AF.Identity),
                   reads=[Bps[b]], writes=[Bbig[c]])
            linear_fm(4, lambda kc: u_t[:, kc, :], lambda kc: [Bu[kc]], evac_k)
            op("pool", (lambda Rt: lambda e: e.dma_start(out=KT[:, :, Rt:Rt + TT].rearrange("h p r -> p h r"), in_=Kst_v))(Rt),
               reads=Bbig[0:16], writes=[Bkd[ti]], chan=ch_ks)
            if ti == 0:
                dbg("Kst", Kst_v, Bbig[0:16], [128, 16, 512], BF16)
            for g4 in range(4):
                wt, bwt = next_w()
                for tb in range(4):
                    b = next_bank()
                    for kc in range(DC):
                        op("pe", (lambda wt, kc, tb, b: lambda e: e.matmul(ps[b], lhsT=u_t[:, kc, tb * 128:(tb + 1) * 128], rhs=wt[:, kc, :],
                                                                           start=(kc == 0), stop=(kc == DC - 1)))(wt, kc, tb, b),
                           reads=[bwt, Bu[kc]], writes=[Bps[b]])
                    if tb % 2 == 0:
                        op("act", (lambda tb, g4, b: lambda e: e.activation(out=Vst_v[:, tb, g4 * 512:(g4 + 1) * 512], in_=ps[b], func=AF.Identity))(tb, g4, b),
                           reads=[Bps[b]], writes=Bbig[16 + 4 * tb + g4:16 + 4 * tb + g4 + 1])
                    else:
                        op("dve", (lambda tb, g4, b: lambda e: e.tensor_copy(Vst_v[:, tb, g4 * 512:(g4 + 1) * 512], ps[b]))(tb, g4, b),
                           reads=[Bps[b]], writes=Bbig[16 + 4 * tb + g4:16 + 4 * tb + g4 + 1])
            for tb in range(4):
                blk = blk_lo + tb
                op("pool", (lambda tb, blk: lambda e: e.dma_start(out=VS[:, :, blk, :].rearrange("h p d -> p h d"),
                                                                  in_=Vst_v[:, tb, :].rearrange("p (h d) -> p h d", d=128)))(tb, blk),
                   reads=Bbig[16 + 4 * tb:16 + 4 * tb + 4], writes=[Bvd[ti]], chan=ch_vs)

            prenorm(G_MIXPRE1, reuse_rstd=True)

            def evac_q(c, b):
                op("act", lambda e: e.activation(out=QT_v[:, c, :], in_=ps[b], func=AF.Identity),
                   reads=[Bps[b]], writes=[Bbig[c]])
            linear_fm(4, lambda kc: u_t[:, kc, :], lambda kc: [Bu[kc]], evac_q)

            nvb = Lt // 128
            if ti == 0:
                dbg("QT", QT_v, Bbig[0:16], [128, 16, 512], BF16)
                dbg("h1", h_t[:], Bh, [128, 16, 512], F32)

            def load_head(hh, slot, Rt=Rt, Lt=Lt, nvb=nvb, blk_lo=blk_lo, ti=ti):
                op("sp", lambda e: e.dma_start(out=KTh_v[slot][:, 0:Lt], in_=KT[hh, :, Rt:S]),
                   reads=Bkd[0:ti + 1], writes=BKTh[slot], chan=ch_kl[slot])
                op("sp", lambda e: e.dma_start(out=Vh_v[slot][:, 0:nvb, :], in_=VS[hh, :, blk_lo:NBLK, :]),
                   reads=Bvd[0:ti + 1], writes=BVh[slot], chan=ch_vl[slot])

            units = []
            for hh in range(NH):
                for qb in range(4):
                    k0 = 128 * qb
                    L = Lt - k0
                    npieces = (L + 511) // 512
                    for g0p in range(0, npieces, 2):
                        gp = min(2, npieces - g0p)
                        g0 = g0p * 512
                        gn = min(L - g0, 1024)
                        units.append(dict(hh=hh, slot=hh % 2, qb=qb, k0=k0, L=L, g0=g0, gp=gp, gn=gn,
                                          first=(g0p == 0), last=(g0 + gn >= L), zb=2 * (len(units) % 2), wo=1024 * (len(units) % 4),
                                          pk=len(units) % 3))
            tslot = [0]

            def stageA(u):
                hh, slot, k0, L, g0, gp, gn = u["hh"], u["slot"], u["k0"], u["L"], u["g0"], u["gp"], u["gn"]
                zb = u["zb"]
                zflat = ps_all[:, zb:zb + 2, :].rearrange("p a b -> p (a b)")
                qT = QT_v[:, hh, k0:k0 + 128]
                kth = KTh_v[slot]
                for pi in range(gp):
                    c0 = g0 + pi * 512
                    n = min(512, L - c0)
                    diag = (c0 == 0)
                    op("pe", (lambda pi, c0, n, diag: lambda e: e.matmul(ps[zb + pi][:, 0:n], lhsT=qT, rhs=kth[:, k0 + c0:k0 + c0 + n],
                                                                         start=True, stop=not diag))(pi, c0, n, diag),
                       reads=[Bbig[hh]] + BKTh[slot], writes=[Bps[zb + pi]])
                    if diag:
                        op("pe", lambda e: e.matmul(ps[zb][:, 0:128], lhsT=ident, rhs=maskm, start=False, stop=True),
                           reads=[Bcst], writes=[Bps[zb]])
                wo = u["wo"]
                op("act", lambda e: e.activation(out=uflat32[:, wo:wo + gn], in_=zflat[:, 0:gn], func=AF.Sigmoid, scale=-SCALE),
                   reads=Bps[zb:zb + gp], writes=bw(wo, wo + gn))

            def stageB1(u):
                gn, pk, wo = u["gn"], u["pk"], u["wo"]
                Pk = P_r[pk]
                if u["first"]:
                    op("dve", lambda e: e.memset(Pk[:, 0:1], 1.0), writes=[BP_r[pk][0]])
                    init = 1.0
                    rd = []
                else:
                    Pp = P_r[(pk - 1) % 3]
                    op("dve", lambda e: e.tensor_copy(Pk[:, 0:1], Pp[:, 1024:1025]), reads=[BP_r[(pk - 1) % 3][2]], writes=[BP_r[pk][0]])
                    init = Pp[:, 1024:1025]
                    rd = [BP_r[(pk - 1) % 3][2]]
                op("dve", lambda e: e.tensor_tensor_scan(out=Pk[:, 1:1 + gn], data0=uflat32[:, wo:wo + gn],
                                                         data1=uflat32[:, wo:wo + gn], initial=init,
                                                         op0=ALU.mult, op1=ALU.min),
                   reads=bw(wo, wo + gn) + rd, writes=BP_r[pk])

            def stageB2(u):
                gn, pk = u["gn"], u["pk"]
                Pk = P_r[pk]
                op("dve", lambda e: e.tensor_tensor(out=a_r[pk][:, 0:gn], in0=Pk[:, 0:gn], in1=Pk[:, 1:gn + 1], op=ALU.subtract),
                   reads=BP_r[pk], writes=Ba_r[pk])

            def stageC(u):
                hh, slot, qb, k0, L, g0, gn = u["hh"], u["slot"], u["qb"], u["k0"], u["L"], u["g0"], u["gn"]
                vh = Vh_v[slot]
                nkb = gn // 128
                for t0 in range(0, nkb, 8):
                    tn = min(8, nkb - t0)
                    tb_ = 4 + (tslot[0] % 2)
                    at = aT_v[tslot[0] % 2]
                    bat = BaT[tslot[0] % 2]
                    tslot[0] += 1
                    ptv = ps[tb_].bitcast(BF16).rearrange("p (a b) -> p a b", b=128)
                    for i in range(tn):
                        cc = (t0 + i) * 128
                        op("pe", (lambda i, cc, ptv, ak: lambda e: e.transpose(ptv[:, i, :], ak[:, cc:cc + 128], ident))(i, cc, ptv, a_r[u["pk"]]),
                           reads=Ba_r[u["pk"]] + [Bcst], writes=[Bps[tb_]])
                    op("act", (lambda at, ptv, tn: lambda e: e.activation(out=at[:, 0:tn, :], in_=ptv[:, 0:tn, :], func=AF.Identity))(at, ptv, tn),
                       reads=[Bps[tb_]], writes=bat)
                    for i in range(tn):
                        kb = t0 + i
                        vb = (k0 + g0) // 128 + kb
                        fa = (u["first"] and kb == 0)
                        la = (g0 + (kb + 1) * 128 >= L)
                        op("pe", (lambda i, vb, fa, la, at: lambda e: e.matmul(ps[6][:, k0:k0 + 128], lhsT=vh[:, vb, :], rhs=at[:, i, :],
                                                                               start=fa, stop=la))(i, vb, fa, la, at),
                           reads=BVh[slot] + bat, writes=[Bps[6]])
                if u["last"] and qb == 3:
                    op("act", lambda e: e.activation(out=OT_v[:, hh, :], in_=ps[6], func=AF.Identity),
                       reads=[Bps[6]], writes=[Bbig[16 + hh]])

            load_head(0, 0)
            stageA(units[0])
            stageA(units[1])
            stageA(units[2])
            stageB1(units[0])
            for i, u in enumerate(units):
                if u["qb"] == 0 and u["first"] and u["hh"] + 1 < NH:
                    load_head(u["hh"] + 1, (u["hh"] + 1) % 2)
                if i + 3 < len(units):
                    stageA(units[i + 3])
                if i + 1 < len(units):
                    stageB1(units[i + 1])
                stageB2(u)
                stageC(u)
            if ti == 0:
                dbg("OT", OT_v, BOT, [128, 16, 512], BF16)
            bank_rr["i"] = 0
            linear_fm(4, lambda kc: OT_v[:, kc, :], lambda kc: [Bbig[16 + kc]], evac_postnorm(G_MIXPOST1))
            residual_update()
            def store_chunks(c, Rt=Rt, ti=ti):
                if c % 4 == 3:
                    g = c // 4
                    op("pool", lambda e: e.dma_start(out=outT[4 * g:4 * g + 4, :, Rt:Rt + TT].rearrange("c p r -> p c r"),
                                                     in_=h_t[:, 4 * g:4 * g + 4, :]),
                       reads=Bh[4 * g:4 * g + 4], writes=[Bout[ti]], chan=ch_os[g])
            mlp(1, next_stats=False, after_chunk=store_chunks)
        S_.wait_events("pool", [("c", c, c.count) for c in ch_os] + [("c", c, c.count) for c in dbg_chans])
        S_.emit()
    return nc


def _host_consts():
    c = np.zeros((128, 384), np.float32)
    c[:, 0:128] = np.eye(128)
    c[:, 128:256] = 1.0
    rt = np.arange(128)[:, None]
    rs = np.arange(128)[None, :]
    c[:, 256:384] = np.where(rs <= rt, NEG, 0.0)
    return c.astype(ml_dtypes.bfloat16)


def _gvec(mix_pre_g, mix_post_g, mlp_pre_g, mlp_post_g, kv_norm_g, a_conv_w):
    vecs = [mix_pre_g[0], mix_pre_g[1], mix_post_g[0], mix_post_g[1], mlp_pre_g[0], mlp_pre_g[1],
            mlp_post_g[0], mlp_post_g[1], kv_norm_g, a_conv_w[0, 0], a_conv_w[0, 1], a_conv_w[0, 2]]
    g = np.stack([np.asarray(v, np.float32).reshape(DC, 128).T for v in vecs], axis=1)
    return np.ascontiguousarray(g.reshape(128, NVEC * DC))


_NC_CACHE = {}


def run_cores(x, a_w_in, a_conv_w, a_w_out, kv_norm_g, w_kv, b_w_q, b_w_o,
              mix_pre_g, mix_post_g, mlp_pre_g, mlp_post_g, mlp_w_up, mlp_w_down):
    x = np.asarray(x, np.float32)
    B, S, _ = x.shape
    NT = S // TT
    if NT not in _NC_CACHE:
        _NC_CACHE[NT] = build_nc(NT)
    nc = _NC_CACHE[NT]
    shared = {
        "gvec": _gvec(*(np.asarray(v, np.float32) for v in (mix_pre_g, mix_post_g, mlp_pre_g, mlp_post_g, kv_norm_g, a_conv_w))),
        "consts": _host_consts(),
        "w_in": np.ascontiguousarray(np.asarray(a_w_in, np.float32)[0]),
        "w_out": np.ascontiguousarray(np.asarray(a_w_out, np.float32)[0]),
        "w_kv": np.ascontiguousarray(np.asarray(w_kv, np.float32)),
        "w_q": np.ascontiguousarray(np.asarray(b_w_q, np.float32)[0]),
        "w_o": np.ascontiguousarray(np.asarray(b_w_o, np.float32)[0]),
        "w_up": np.ascontiguousarray(np.asarray(mlp_w_up, np.float32)),
        "w_dn": np.ascontiguousarray(np.asarray(mlp_w_down, np.float32)),
    }
    in_maps = []
    for b in range(B):
        xr = np.ascontiguousarray(x[b, ::-1, :].T).reshape(DC, 128, S)
        m = dict(shared)
        m["xin"] = xr
        in_maps.append(m)
    res = run_bass_kernel_spmd(nc, in_maps, core_ids=list(range(B)))
    global LAST_RES
    LAST_RES = res.results
    out = np.empty((B, S, D), np.float32)
    for b in range(B):
        o = res.results[b]["outT"].reshape(D, S)
        out[b] = o.T[::-1, :]
    return out


def kernel(**inputs):
    return run_cores(**inputs)
```
